# Optimizing a Trainium2 kernel written in Bass

```python
import math
import jax, jax.numpy as jnp
from jax import lax
import numpy as np

D_MODEL = 1024
BATCH = 2
SEQ = 8192
DEPTH = 1

MEM_LEN = 256
MLA_NOPE = 128
MLA_ROPE = 64
MLA_V = 128
MLA_HEADS = D_MODEL // MLA_V
MLA_KV_RANK = 256
GLA_HEADS = 4
GLA_DK = D_MODEL // 2 // GLA_HEADS
GLA_DV = D_MODEL // GLA_HEADS
GLA_GATE_RANK = 16
GLA_GATE_NORMALIZER = 16.0
GLA_CHUNK = 64
MEM_HEADS = 4
MEM_DQK = 64
MEM_DV = D_MODEL // MEM_HEADS
N_BRANCHES = 3
D_FF = 4 * D_MODEL
ATTN_BLOCK = 128
ROPE_THETA = 10000.0
EPS = 1e-6

IN_SPLITS = (
    MLA_HEADS * (MLA_NOPE + MLA_ROPE),
    MLA_KV_RANK,
    MLA_ROPE,
    GLA_HEADS * GLA_DK,
    GLA_HEADS * GLA_DK,
    GLA_HEADS * GLA_DV,
    GLA_GATE_RANK,
    GLA_HEADS * GLA_DV,
    MEM_HEADS * MEM_DQK,
    N_BRANCHES * D_MODEL,
)
D_IN = sum(IN_SPLITS)

kernel_name = 'hybrid_mla_gla_mem_block'


def rms_norm(t, g):
    tf = t.astype(jnp.float32)
    y = tf * lax.rsqrt(jnp.mean(tf * tf, axis=-1, keepdims=True) + EPS)
    return (y * g.astype(jnp.float32)).astype(t.dtype)


def rope_tables(positions):
    inv = 1.0 / (ROPE_THETA ** (jnp.arange(0, MLA_ROPE, 2, dtype=jnp.float32) / MLA_ROPE))
    ang = positions.astype(jnp.float32)[..., None] * inv
    return jnp.cos(ang), jnp.sin(ang)


def apply_rope(t, cos, sin):
    tf = t.astype(jnp.float32)
    t1, t2 = jnp.split(tf, 2, axis=-1)
    c = cos[:, :, None, :]
    s = sin[:, :, None, :]
    return jnp.concatenate([t1 * c - t2 * s, t1 * s + t2 * c], axis=-1).astype(t.dtype)


def causal_block_attention(q, k, v, scale):
    B, S, H, D = q.shape
    Dv = v.shape[-1]
    nb = S // ATTN_BLOCK
    qb = q.reshape(B, nb, ATTN_BLOCK, H, D).transpose(1, 0, 3, 2, 4)
    kh = k.transpose(0, 2, 1, 3)
    vh = v.transpose(0, 2, 1, 3)
    kpos = jnp.arange(S)

    def one_block(args):
        qi, i = args
        s = jnp.einsum('bhqd,bhkd->bhqk', qi, kh).astype(jnp.float32) * scale
        qpos = i * ATTN_BLOCK + jnp.arange(ATTN_BLOCK)
        s = jnp.where(kpos[None, :] <= qpos[:, None], s, -1e30)
        p = jax.nn.softmax(s, axis=-1).astype(vh.dtype)
        return jnp.einsum('bhqk,bhkd->bhqd', p, vh)

    o = lax.map(one_block, (qb, jnp.arange(nb)))
    return o.transpose(1, 0, 3, 2, 4).reshape(B, S, H, Dv)


def gla_chunked(q, k, v, log_a):
    B, S, H, DK = q.shape
    DV = v.shape[-1]
    C = GLA_CHUNK
    nc = S // C

    def to_chunks(t):
        return t.astype(jnp.float32).reshape(B, nc, C, H, t.shape[-1]).transpose(1, 0, 3, 2, 4)

    qc, kc, vc = to_chunks(q), to_chunks(k), to_chunks(v)
    bc = jnp.cumsum(to_chunks(log_a), axis=3)
    mask = jnp.tril(jnp.ones((C, C), dtype=bool))[None, None, :, :, None]

    def step(state, inp):
        qi, ki, vi, bi = inp
        diff = bi[:, :, :, None, :] - bi[:, :, None, :, :]
        decay = jnp.where(mask, jnp.exp(jnp.where(mask, diff, 0.0)), 0.0)
        a = jnp.einsum('bhid,bhjd,bhijd->bhij', qi, ki, decay)
        o_intra = jnp.einsum('bhij,bhje->bhie', a, vi)
        o_inter = jnp.einsum('bhid,bhde->bhie', qi * jnp.exp(bi), state)
        b_last = bi[:, :, -1, :]
        k_dec = ki * jnp.exp(b_last[:, :, None, :] - bi)
        new_state = jnp.exp(b_last)[..., None] * state + jnp.einsum('bhjd,bhje->bhde', k_dec, vi)
        return new_state, o_intra + o_inter

    state0 = jnp.zeros((B, H, DK, DV), jnp.float32)
    _, o = lax.scan(step, state0, (qc, kc, vc, bc))
    return o.transpose(1, 0, 3, 2, 4).reshape(B, S, H, DV).astype(q.dtype)


def hybrid_layer(x, mem, cos, sin, g_mix, w_in, b_gate, g_ckv, w_ukv, g_q_nope, g_k_nope,
                 g_q_rope, g_k_rope, w_gla_gate, b_gla_gate, g_gla_out, g_mem, w_mem_kv,
                 g_q_mem, g_k_mem, w_o, g_ffn, w_up, w_down):
    B, S, _ = x.shape
    M = mem.shape[1]
    h = rms_norm(x, g_mix)
    proj = h @ w_in
    offsets = []
    acc = 0
    for n in IN_SPLITS[:-1]:
        acc += n
        offsets.append(acc)
    (q_mla, c_kv, k_rope, q_gla, k_gla, v_gla, a_gla, r_gla, q_mem,
     gate_logits) = jnp.split(proj, offsets, axis=-1)

    q_mla = q_mla.reshape(B, S, MLA_HEADS, MLA_NOPE + MLA_ROPE)
    q_nope = rms_norm(q_mla[..., :MLA_NOPE], g_q_nope)
    q_pe = apply_rope(rms_norm(q_mla[..., MLA_NOPE:], g_q_rope), cos, sin)
    kv = (rms_norm(c_kv, g_ckv) @ w_ukv).reshape(B, S, MLA_HEADS, MLA_NOPE + MLA_V)
    k_nope = rms_norm(kv[..., :MLA_NOPE], g_k_nope)
    v_mla = kv[..., MLA_NOPE:]
    k_pe = apply_rope(rms_norm(k_rope, g_k_rope)[:, :, None, :], cos, sin)
    q_full = jnp.concatenate([q_nope, q_pe], axis=-1)
    k_full = jnp.concatenate([k_nope, jnp.broadcast_to(k_pe, (B, S, MLA_HEADS, MLA_ROPE))], axis=-1)
    o_mla = causal_block_attention(q_full, k_full, v_mla,
                                   (MLA_NOPE + MLA_ROPE) ** -0.5).reshape(B, S, MLA_HEADS * MLA_V)

    qg = q_gla.reshape(B, S, GLA_HEADS, GLA_DK) * (GLA_DK ** -0.5)
    kg = k_gla.reshape(B, S, GLA_HEADS, GLA_DK)
    vg = v_gla.reshape(B, S, GLA_HEADS, GLA_DV)
    log_a = jax.nn.log_sigmoid((a_gla @ w_gla_gate + b_gla_gate).astype(jnp.float32)) / GLA_GATE_NORMALIZER
    log_a = log_a.reshape(B, S, GLA_HEADS, GLA_DK)
    og = gla_chunked(qg, kg, vg, log_a)
    o_gla = rms_norm(og, g_gla_out).reshape(B, S, GLA_HEADS * GLA_DV) * jax.nn.silu(r_gla)

    kv_m = rms_norm(mem, g_mem) @ w_mem_kv
    k_m = rms_norm(kv_m[..., :MEM_HEADS * MEM_DQK].reshape(B, M, MEM_HEADS, MEM_DQK), g_k_mem)
    v_m = kv_m[..., MEM_HEADS * MEM_DQK:].reshape(B, M, MEM_HEADS, MEM_DV)
    q_m = rms_norm(q_mem.reshape(B, S, MEM_HEADS, MEM_DQK), g_q_mem)
    s_m = jnp.einsum('bshd,bmhd->bhsm', q_m, k_m).astype(jnp.float32) * (MEM_DQK ** -0.5)
    p_m = jax.nn.softmax(s_m, axis=-1).astype(v_m.dtype)
    o_mem = jnp.einsum('bhsm,bmhd->bshd', p_m, v_m).reshape(B, S, MEM_HEADS * MEM_DV)

    gates = jax.nn.sigmoid(gate_logits + b_gate).reshape(B, S, N_BRANCHES, D_MODEL)
    y = gates[..., 0, :] * o_mla + gates[..., 1, :] * o_gla + gates[..., 2, :] * o_mem
    x = x + y @ w_o

    h2 = rms_norm(x, g_ffn)
    return x + jnp.square(jax.nn.relu(h2 @ w_up)) @ w_down


def setup_inputs(seed: int = 0) -> dict:
    key = jax.random.key(seed)
    ks = jax.random.split(key, 24)
    L = DEPTH

    def w(k, shape, fan_in):
        return jax.random.normal(k, (L,) + shape, jnp.float32) * (fan_in ** -0.5)

    def gain(k, n):
        return 1.0 + 0.02 * jax.random.normal(k, (L, n), jnp.float32)

    def bias(k, n):
        return 0.01 * jax.random.normal(k, (L, n), jnp.float32)

    x = jax.random.normal(ks[0], (BATCH, SEQ, D_MODEL), jnp.float32)
    mem = jax.random.normal(ks[1], (BATCH, MEM_LEN, D_MODEL), jnp.float32)
    positions = (jax.random.randint(ks[2], (BATCH, 1), 0, 4096, dtype=jnp.int32)
                 + jnp.arange(SEQ, dtype=jnp.int32)[None, :])
    return {
        'x': x,
        'mem': mem,
        'positions': positions,
        'g_mix': gain(ks[3], D_MODEL),
        'w_in': w(ks[4], (D_MODEL, D_IN), D_MODEL),
        'b_gate': bias(ks[5], N_BRANCHES * D_MODEL),
        'g_ckv': gain(ks[6], MLA_KV_RANK),
        'w_ukv': w(ks[7], (MLA_KV_RANK, MLA_HEADS * (MLA_NOPE + MLA_V)), MLA_KV_RANK),
        'g_q_nope': gain(ks[8], MLA_NOPE),
        'g_k_nope': gain(ks[9], MLA_NOPE),
        'g_q_rope': gain(ks[10], MLA_ROPE),
        'g_k_rope': gain(ks[11], MLA_ROPE),
        'w_gla_gate': w(ks[12], (GLA_GATE_RANK, GLA_HEADS * GLA_DK), GLA_GATE_RANK),
        'b_gla_gate': bias(ks[13], GLA_HEADS * GLA_DK),
        'g_gla_out': gain(ks[14], GLA_DV),
        'g_mem': gain(ks[15], D_MODEL),
        'w_mem_kv': w(ks[16], (D_MODEL, MEM_HEADS * (MEM_DQK + MEM_DV)), D_MODEL),
        'g_q_mem': gain(ks[17], MEM_DQK),
        'g_k_mem': gain(ks[18], MEM_DQK),
        'w_o': w(ks[19], (D_MODEL, D_MODEL), D_MODEL),
        'g_ffn': gain(ks[20], D_MODEL),
        'w_up': w(ks[21], (D_MODEL, D_FF), D_MODEL),
        'w_down': w(ks[22], (D_FF, D_MODEL), D_FF),
    }


def reference(x, mem, positions, g_mix, w_in, b_gate, g_ckv, w_ukv, g_q_nope, g_k_nope,
              g_q_rope, g_k_rope, w_gla_gate, b_gla_gate, g_gla_out, g_mem, w_mem_kv,
              g_q_mem, g_k_mem, w_o, g_ffn, w_up, w_down):
    cos, sin = rope_tables(positions)
    for l in range(DEPTH):
        x = hybrid_layer(x, mem, cos, sin, g_mix[l], w_in[l], b_gate[l], g_ckv[l], w_ukv[l],
                         g_q_nope[l], g_k_nope[l], g_q_rope[l], g_k_rope[l], w_gla_gate[l],
                         b_gla_gate[l], g_gla_out[l], g_mem[l], w_mem_kv[l], g_q_mem[l],
                         g_k_mem[l], w_o[l], g_ffn[l], w_up[l], w_down[l])
    return x
```

```python
from contextlib import ExitStack

import numpy as np
import ml_dtypes

import concourse.bass as bass
import concourse.mybir as mybir
from concourse.bass_utils import run_bass_kernel_spmd

F32 = mybir.dt.float32
BF16 = mybir.dt.bfloat16
I32 = mybir.dt.int32
AF = mybir.ActivationFunctionType
ALU = mybir.AluOpType

D = 1024
B = 2
S = 8192
DFF = 4096
EPS = 1e-6
NCORES = 8


class Buf:
    __slots__ = ("w", "r", "name", "excl")

    def __init__(self, name="", excl=False):
        self.w = []
        self.r = {}
        self.name = name
        self.excl = excl


class Prog:
    LIMIT = 6000
    NDMA = 6
    EMBED = False

    def __init__(self, nc, stack):
        self.nc = nc
        self.stack = stack
        self.nsem = 0
        self.E = {}
        for name, h in (("pe", nc.tensor), ("act", nc.scalar), ("dve", nc.vector),
                        ("pool", nc.gpsimd), ("sp", nc.sync)):
            self.E[name] = {"h": h, "sem": None, "cnt": 0, "seen": {}, "key": None,
                            "slots": None, "rr": 0}
        self.stores = []
        self.stopped = False
        self.extra_toks = []

    def new_sem(self, name):
        self.nsem += 1
        s = self.stack.enter_context(self.nc.semaphore(f"{name}_{self.nsem}"))
        return (f"{name}_{self.nsem}", s)

    def _wait(self, eng, deps, defer_last=False):
        e = self.E[eng]
        best = {}
        for t in deps:
            if t is None:
                continue
            key, sem, val, src = t
            if eng == "pe" and src == "pe":
                continue
            if e["seen"].get(key, 0) >= val:
                continue
            if key not in best or best[key][1] < val:
                best[key] = (sem, val)
        items = list(best.items())
        last = None
        if defer_last and items:
            last = items.pop()
        for key, (sem, val) in items:
            e["h"].wait_ge(sem, val)
            e["seen"][key] = val
        if last is not None:
            e["seen"][last[0]] = last[1][1]
            return last[1]
        return None

    def _deps(self, reads, writes):
        deps = []
        for b in reads:
            deps.extend(b.w)
            if b.excl:
                deps.extend(b.r.values())
        for b in writes:
            deps.extend(b.w)
            deps.extend(b.r.values())
        return deps

    def _mark(self, tok, rkey, reads, writes):
        for b in reads:
            if b.excl:
                b.w = [tok]
                b.r = {}
            else:
                b.r[rkey] = tok
        for b in writes:
            if tok[3] == "dma" and not b.r and all(t[3] == "dma" for t in b.w):
                b.w = b.w + [tok]
            else:
                b.w = [tok]
            b.r = {}

    def op(self, eng, fn, reads=(), writes=(), signal=True):
        if self.stopped:
            return None
        e = self.E[eng]
        emb = self._wait(eng, self._deps(reads, writes), defer_last=self.EMBED)
        if e["sem"] is None:
            e["key"], e["sem"] = self.new_sem(eng)
            e["cnt"] = 0
        ins = fn(e["h"])
        if emb is not None:
            ins.wait_op(emb[0], emb[1], "sem-ge")
        if signal:
            e["cnt"] += 1
            ins.then_inc(e["sem"], 1)
            tok = (e["key"], e["sem"], e["cnt"], eng)
            if e["cnt"] >= self.LIMIT:
                e["sem"] = None
        else:
            tok = (e["key"], e["sem"], e["cnt"] + 1, eng)
        self._mark(tok, eng, reads, writes)
        return tok

    def dma(self, queue, out, in_, reads=(), writes=(), is_store=False, **kw):
        if self.stopped:
            return None
        e = self.E[queue]
        if e["slots"] is None:
            e["slots"] = []
            for i in range(self.NDMA):
                key, sem = self.new_sem(f"d{queue}")
                e["slots"].append({"key": key, "sem": sem, "cnt": 0, "last": None})
        slot = e["slots"][e["rr"] % self.NDMA]
        e["rr"] += 1
        deps = self._deps(reads, writes)
        deps.append(slot["last"])
        self._wait(queue, deps)
        ins = e["h"].dma_start(out=out, in_=in_, **kw)
        slot["cnt"] += 16
        ins.then_inc(slot["sem"], 16)
        tok = (slot["key"], slot["sem"], slot["cnt"], "dma")
        slot["last"] = tok
        self._mark(tok, slot["key"], reads, writes)
        if is_store:
            self.stores.append(tok)
        return tok

    def barrier(self):
        toks = []
        for name, e in self.E.items():
            if e["sem"] is not None and e["cnt"] > 0:
                toks.append((e["key"], e["sem"], e["cnt"], name))
            if e["slots"] is not None:
                for sl in e["slots"]:
                    if sl["last"] is not None:
                        toks.append(sl["last"])
        for name in self.E:
            e = self.E[name]
            best = {}
            for key, sem, val, src in toks:
                if e["seen"].get(key, 0) >= val:
                    continue
                if key not in best or best[key][1] < val:
                    best[key] = (sem, val)
            for key, (sem, val) in best.items():
                e["h"].wait_ge(sem, val)
                e["seen"][key] = val

    def collective(self, kind, ins, outs, groups, reads=(), writes=()):
        e = self.E["pool"]
        self._wait("pool", self._deps(reads, writes))
        key, sem = self.new_sem("cc")
        ins_ = e["h"].collective_compute(kind, ALU.bypass, replica_groups=groups,
                                         ins=[a.opt() for a in ins], outs=[a.opt() for a in outs])
        ins_.then_inc(sem)
        tok = (key, sem, 1, "dma")
        self.extra_toks.append(tok)
        self._mark(tok, key, reads, writes)
        return tok

    def finish(self):
        self.stopped = False
        self._wait("sp", self.stores)


class PsumPool:
    def __init__(self, nc, stack, n=8, tag=""):
        self.t = [stack.enter_context(nc.psum_tensor(f"ps{tag}{i}", [128, 512], F32)) for i in range(n)]
        self.b = [Buf(f"ps{i}", True) for i in range(n)]
        self.i = 0
        self.active = list(range(n))

    def get(self):
        i = self.active[self.i % len(self.active)]
        self.i += 1
        self.last = i
        return self.t[i], self.b[i]


TOK2 = B * S // NCORES
TT2 = 256


def emit_phase_c(nc, P, st, y_src, y_queue, xT, w_o, w_up, w_dn, g_ffn, oT, y_reads=(), pre=None):
    KD = D // 128
    KF = DFF // 128
    NT = TOK2 // TT2
    if True:
        sb = lambda name, shape, dt: st.enter_context(nc.sbuf_tensor("c_" + name, shape, dt))
        wo_sb = sb("wo_sb", [128, KD, D], BF16)
        wu_sb = sb("wu_sb", [128, KD, DFF], BF16)
        wd_sb = sb("wd_sb", [128, KF, D], BF16)
        g_sb = sb("g_sb", [128, KD], F32)
        ones = sb("ones", [128, 128], BF16)
        y_sb = [sb(f"y_sb{i}", [128, KD, TT2], BF16) for i in range(2)]
        x_sb = [sb(f"x_sb{i}", [128, KD, TT2], F32) for i in range(2)]
        h_sb = [sb(f"h_sb{i}", [128, KD, TT2], BF16) for i in range(2)]
        u_sb = sb("u_sb", [128, KF, TT2], BF16)
        sq_sb = [sb(f"sq_sb{i}", [128, TT2], BF16) for i in range(3)]
        r_sb = [sb(f"r_sb{i}", [128, TT2], F32) for i in range(3)]
        rstd = sb("rstd", [128, TT2], F32)
        ps = PsumPool(nc, st, 7, "c")
        ss_t = st.enter_context(nc.psum_tensor("c_ss_ps", [128, 512], F32))
        ss_b = Buf("ss", True)
        b_wo = [Buf() for _ in range(KD)]
        b_wu = [Buf() for _ in range(KD)]
        b_wd = [Buf() for _ in range(KF)]
        b_g, b_ones = Buf(), Buf()
        b_y = [Buf(), Buf()]
        b_x = [[Buf() for _ in range(KD)] for _ in range(2)]
        b_h = [[Buf() for _ in range(KD)] for _ in range(2)]
        b_u = [Buf() for _ in range(KF)]
        b_sq = [Buf(), Buf(), Buf()]
        b_r = [Buf(), Buf(), Buf()]
        b_rstd = Buf()

        P.op("dve", lambda e: e.memset(ones[:], 1.0), writes=[b_ones])
        P.dma("sp", g_sb[:], g_ffn, writes=[b_g])

        def load_tile(t):
            i = t % 2
            c0 = t * TT2
            P.dma(y_queue, y_sb[i][:], y_src(t), reads=(y_reads(t) if callable(y_reads) else list(y_reads)),
                  writes=[b_y[i]])
            P.dma("sp", x_sb[i][:], xT[:, c0:c0 + TT2].rearrange("(k p) c -> p k c", p=128),
                  writes=b_x[i])

        if pre is None:
            for k in range(KD):
                P.dma("pool", wo_sb[:, k, :], w_o[k * 128:(k + 1) * 128, :], writes=[b_wo[k]])
            load_tile(0)
            for k in range(KD):
                for hh in range(2):
                    P.dma("pool", wu_sb[:, k, hh * 2048:(hh + 1) * 2048],
                          w_up[k * 128:(k + 1) * 128, hh * 2048:(hh + 1) * 2048], writes=[b_wu[k]])
            for f in range(KF):
                P.dma("pool", wd_sb[:, f, :], w_dn[f * 128:(f + 1) * 128, :], writes=[b_wd[f]])
            wu_rd = lambda k, f: b_wu[k]
            wd_rd = lambda f: b_wd[f]
        else:
            wo_b, wu_b, wd_b, bo_, bu_, bd_ = pre
            P.dma("sp", wo_sb[:], wo_b.rearrange("(k p) n -> p k n", p=128), reads=[bo_], writes=[b_wo[0]])
            for k in range(1, KD):
                b_wo[k] = b_wo[0]
            load_tile(0)
            b_wug = [Buf() for _ in range(8)]
            b_wdg = [Buf() for _ in range(8)]
            for cg in range(8):
                P.dma("sp", wu_sb[:, :, cg * 512:(cg + 1) * 512],
                      wu_b[:, cg * 512:(cg + 1) * 512].rearrange("(k p) n -> p k n", p=128),
                      reads=[bu_], writes=[b_wug[cg]])
            for g_ in range(8):
                P.dma("sp", wd_sb[:, g_ * 4:(g_ + 1) * 4, :],
                      wd_b[g_ * 512:(g_ + 1) * 512, :].rearrange("(f p) n -> p f n", p=128),
                      reads=[bd_], writes=[b_wdg[g_]])
            wu_rd = lambda k, f: b_wug[f // 4]
            wd_rd = lambda f: b_wdg[f // 4]

        def prologue(t):
            i = t % 2
            xs, ys = x_sb[i], y_sb[i]
            lag = []

            def ss_mm(n, j):
                P.op("pe", lambda e: e.matmul(ss_t[:, 0:TT2], ones[:], sq_sb[j][:], start=(n == 0),
                                              stop=(n == KD - 1)),
                     reads=[b_ones, b_sq[j]], writes=[ss_b], signal=True)

            for n in range(KD):
                pt, pb = ps.get()
                for k in range(KD):
                    P.op("pe", lambda e, k=k, n=n, pt=pt: e.matmul(
                        pt[:, 0:TT2], wo_sb[:, k, n * 128:(n + 1) * 128], ys[:, k, :],
                        start=(k == 0), stop=(k == KD - 1)),
                        reads=[b_wo[k], b_y[i]], writes=[pb], signal=(k == KD - 1))
                P.op("dve", lambda e, n=n, pt=pt: e.tensor_tensor(
                    out=xs[:, n, :], in0=xs[:, n, :], in1=pt[:, 0:TT2], op=ALU.add),
                    reads=[pb, b_x[i][n]], writes=[b_x[i][n]])
                j = n % 3
                P.op("act", lambda e, n=n, j=j: e.activation(
                    out=sq_sb[j][:], in_=xs[:, n, :], func=AF.Square),
                    reads=[b_x[i][n]], writes=[b_sq[j]])
                lag.append((n, j))
                if len(lag) > 2:
                    ss_mm(*lag.pop(0))
            while lag:
                ss_mm(*lag.pop(0))
            P.op("act", lambda e: e.activation(out=rstd[:], in_=ss_t[:, 0:TT2], func=AF.Ln,
                                               scale=1.0 / D, bias=EPS),
                 reads=[ss_b], writes=[b_rstd])
            P.op("act", lambda e: e.activation(out=rstd[:], in_=rstd[:], func=AF.Exp, scale=-0.5),
                 reads=[b_rstd], writes=[b_rstd])
            for k in range(KD):
                P.op("dve", lambda e, k=k: e.scalar_tensor_tensor(
                    out=h_sb[i][:, k, :], in0=xs[:, k, :], scalar=g_sb[:, k:k + 1], in1=rstd[:],
                    op0=ALU.mult, op1=ALU.mult),
                    reads=[b_x[i][k], b_g, b_rstd], writes=[b_h[i][k]])

        def up(t):
            i = t % 2
            for f in range(KF):
                pt, pb = ps.get()
                for k in range(KD):
                    P.op("pe", lambda e, k=k, f=f, pt=pt: e.matmul(
                        pt[:, 0:TT2], wu_sb[:, k, f * 128:(f + 1) * 128], h_sb[i][:, k, :],
                        start=(k == 0), stop=(k == KD - 1)),
                        reads=[wu_rd(k, f), b_h[i][k]], writes=[pb], signal=(k == KD - 1))
                j = f % 3
                P.op("act", lambda e, j=j, pt=pt: e.activation(
                    out=r_sb[j][:], in_=pt[:, 0:TT2], func=AF.Relu),
                    reads=[pb], writes=[b_r[j]])
                P.op("dve", lambda e, j=j, f=f, pt=pt: e.tensor_tensor(
                    out=u_sb[:, f, :], in0=r_sb[j][:], in1=pt[:, 0:TT2], op=ALU.mult),
                    reads=[pb, b_r[j]], writes=[b_u[f]])

        def down(t):
            i = t % 2
            xs = x_sb[i]
            for n in range(KD):
                pt, pb = ps.get()
                for f in range(KF):
                    P.op("pe", lambda e, f=f, n=n, pt=pt: e.matmul(
                        pt[:, 0:TT2], wd_sb[:, f, n * 128:(n + 1) * 128], u_sb[:, f, :],
                        start=(f == 0), stop=(f == KF - 1)),
                        reads=[wd_rd(f), b_u[f]], writes=[pb], signal=(f == KF - 1))
                P.op("dve", lambda e, n=n, pt=pt: e.tensor_tensor(
                    out=xs[:, n, :], in0=xs[:, n, :], in1=pt[:, 0:TT2], op=ALU.add),
                    reads=[pb, b_x[i][n]], writes=[b_x[i][n]])
            c0 = t * TT2
            P.dma("sp", oT[:, c0:c0 + TT2].rearrange("(k p) c -> p k c", p=128), xs[:],
                  reads=b_x[i], is_store=True)

        prologue(0)
        for t in range(NT):
            if t + 1 < NT:
                load_tile(t + 1)
            up(t)
            if t + 1 < NT:
                prologue(t + 1)
            down(t)


def build_l2():
    nc = bass.Bass("TRN2", target_bir_lowering=False)
    yT = nc.dram_tensor("yT", [D, TOK2], F32, kind="ExternalInput").ap()
    xT = nc.dram_tensor("xT", [D, TOK2], F32, kind="ExternalInput").ap()
    w_o = nc.dram_tensor("w_o", [D, D], F32, kind="ExternalInput").ap()
    w_up = nc.dram_tensor("w_up", [D, DFF], F32, kind="ExternalInput").ap()
    w_dn = nc.dram_tensor("w_down", [DFF, D], F32, kind="ExternalInput").ap()
    g_ffn = nc.dram_tensor("g_ffn", [128, D // 128], F32, kind="ExternalInput").ap()
    oT = nc.dram_tensor("oT", [D, TOK2], F32, kind="ExternalOutput").ap()
    with ExitStack() as st:
        st.enter_context(nc.Block())
        P = Prog(nc, st)
        y_src = lambda t: yT[:, t * TT2:(t + 1) * TT2].rearrange("(k p) c -> p k c", p=128)
        emit_phase_c(nc, P, st, y_src, "pool", xT, w_o, w_up, w_dn, g_ffn, oT)
        P.finish()
    return nc


def run_l2(yT_full, xT_full, w_o, w_up, w_down, g_ffn):
    nc = build_l2()
    g = np.ascontiguousarray(g_ffn.reshape(D // 128, 128).T)
    in_maps = []
    for c in range(NCORES):
        in_maps.append({"yT": yT_full[c], "xT": xT_full[c], "w_o": w_o, "w_up": w_up,
                        "w_down": w_down, "g_ffn": g})
    res = run_bass_kernel_spmd(nc, in_maps, core_ids=list(range(NCORES)))
    return [r["oT"] for r in res.results]


TT = 512
NTILE = S // TT
NW1 = 12 * 128 + 80 + 6 * 128
NCOL = 38
SC_MLA = 192.0 ** -0.5
SC_MEM = 64.0 ** -0.5
SC_GLA = 128.0 ** -0.5


def _choff(i):
    if i < 12:
        return 128 * i, 128
    if i == 12:
        return 1536, 80
    return 1616 + 128 * (i - 13), 128


class SbPool:
    def __init__(self, nc, stack, name, shape, dt, n):
        self.t = [stack.enter_context(nc.sbuf_tensor(f"{name}{i}", shape, dt)) for i in range(n)]
        self.b = [Buf(f"{name}{i}") for i in range(n)]
        self.i = 0

    def get(self):
        i = self.i % len(self.t)
        self.i += 1
        return self.t[i], self.b[i]


class _Stop(Exception):
    pass


def build_l1(ntile=NTILE, stop=None, fused=False):
    nc = bass.Bass("TRN2", target_bir_lowering=False)

    def ck(n):
        if stop is not None and n >= stop:
            P.stopped = True
    din = lambda name, shape, dt=F32: nc.dram_tensor(name, shape, dt, kind="ExternalInput").ap()
    xT = din("xT", [D, S])
    pos = din("pos", [1, S], I32)
    w1 = din("w1", [D, NW1])
    w1t = din("w1t", [D, 256])
    wukv = din("wukv", [256, 512])
    wg = din("wg", [16, 128])
    memT = din("memT", [D, 256])
    wmem = din("wmem", [D, 320])
    cols_d = din("cols", [128, NCOL])
    tri_d = din("tri", [128, 128])
    ident_d = din("ident", [128, 128])
    if fused:
        xq = din("xq", [D, TOK2])
        w_o = din("w_o", [D, D])
        w_up = din("w_up", [D, DFF])
        w_dn = din("w_down", [DFF, D])
        g_ffn = din("g_ffn", [128, D // 128])
        oT = nc.dram_tensor("oT", [D, TOK2], F32, kind="ExternalOutput").ap()
        wo_b = nc.dram_tensor("wo_b", [D, D], BF16).ap()
        wu_b = nc.dram_tensor("wu_b", [D, DFF], BF16).ap()
        wd_b = nc.dram_tensor("wd_b", [DFF, D], BF16).ap()
        precast = ([(wo_b[k * 128:(k + 1) * 128, :], w_o[k * 128:(k + 1) * 128, :], "wo_b") for k in range(8)]
                   + [(wu_b[k * 128:(k + 1) * 128, hh * 2048:(hh + 1) * 2048],
                       w_up[k * 128:(k + 1) * 128, hh * 2048:(hh + 1) * 2048], "wu_b")
                      for k in range(8) for hh in range(2)]
                   + [(wd_b[k * 128:(k + 1) * 128, :], w_dn[k * 128:(k + 1) * 128, :], "wd_b") for k in range(32)])
        ybuf = [nc.dram_tensor(f"ybuf{q}", [256, TOK2], BF16).ap() for q in range(4)]
        ygath = nc.dram_tensor("ygath", [4 * D, TOK2], BF16).ap()
    else:
        yT = nc.dram_tensor("yT", [256, S], F32, kind="ExternalOutput").ap()
    KD = D // 128
    st0 = ExitStack()
    st0.enter_context(nc.Block())
    P = Prog(nc, st0)
    with ExitStack() as st:
        sb = lambda name, shape, dt: st.enter_context(nc.sbuf_tensor(name, shape, dt))
        w1_sb = sb("w1_sb", [128, KD, NW1], BF16)
        w1t_sb = sb("w1t_sb", [128, KD, 256], BF16)
        wukv_sb = sb("wukv_sb", [128, 2, 512], BF16)
        wg_sb = sb("wg_sb", [128, 128], BF16)
        gw_sb = sb("gw_sb", [128, 8 * TT], BF16)
        cols = sb("cols_sb", [128, NCOL], F32)
        ncols = sb("ncols_sb", [128, NCOL], F32)
        tri = sb("tri_sb", [128, 128], BF16)
        ident = sb("ident_sb", [128, 128], BF16)
        ones = sb("ones_sb", [128, 128], BF16)
        bd64 = sb("bd64_sb", [128, 128], BF16)
        onesf = sb("onesf_sb", [128, 128], F32)
        cache = sb("cache", [128, 5 * S], BF16)
        kn_c = cache[:, 0:2 * S].rearrange("p (h s) -> p h s", h=2)
        kpe_c = cache[:, 2 * S:3 * S]
        v_c = cache[:, 3 * S:5 * S].rearrange("p (b e) -> p b e", e=256)
        stg = cache[:].bitcast(F32)
        x_sb = sb("x_sb", [128, KD, TT], F32)
        h_sb = sb("h_sb", [128, KD, TT], BF16)
        qn_sb = sb("qn_sb", [128, 2, TT], BF16)
        qpe_sb = sb("qpe_sb", [128, 2, TT], BF16)
        ckvn_sb = sb("ckvn_sb", [128, 2, TT], BF16)
        qg_sb = sb("qg_sb", [128, TT], BF16)
        kg_sb = sb("kg_sb", [128, TT], BF16)
        kgf_sb = sb("kgf_sb", [128, TT], F32)
        cum_sb = sb("cum_sb", [128, TT], F32)
        vg_sb = sb("vg_sb", [128, 4, 256], BF16)
        qa_sb = sb("qa_sb", [128, TT], BF16)
        og_sb = sb("og_sb", [128, 2, TT], F32)
        omla_sb = sb("omla_sb", [128, 2, TT], F32)
        S_sb = sb("S_sb", [128, 256], F32)
        Sb_sb = sb("Sb_sb", [128, 256], BF16)
        kmT_sb = sb("kmT_sb", [128, 256], BF16)
        vm_sb = sb("vm_sb", [128, 2, 256], BF16)
        cs_sb = sb("cs_sb", [128, TT], F32)
        sn_sb = sb("sn_sb", [128, TT], F32)
        small = sb("small_sb", [128, 16], F32)
        FP = SbPool(nc, st, "fp", [128, TT], F32, 5)
        HP = SbPool(nc, st, "hp", [128, TT], BF16, 3)
        PT = SbPool(nc, st, "pt", [128, TT], BF16, 4)
        ps = PsumPool(nc, st, 7)
        o_ps = [ps.t[3], ps.t[4]]
        l_ps = [ps.t[5], ps.t[6]]
        tp_ps = st.enter_context(nc.psum_tensor("tp_ps", [128, 1024], BF16))
        b_o = [ps.b[3], ps.b[4]]
        b_l = [ps.b[5], ps.b[6]]
        b_tp = Buf("tp", True)

        B_ = {}

        def bf(name):
            if name not in B_:
                B_[name] = Buf(name)
            return B_[name]

        P.dma("sp", cols[:], cols_d, writes=[bf("cols")])
        P.op("dve", lambda e: e.tensor_scalar(out=ncols[:], in0=cols[:], scalar1=-1.0, scalar2=None,
                                              op0=ALU.mult), reads=[bf("cols")], writes=[bf("ncols")])
        P.op("dve", lambda e: e.memset(ones[:], 1.0), writes=[bf("ones")])
        P.op("dve", lambda e: e.memset(onesf[:], 1.0), writes=[bf("onesf")])
        P.op("dve", lambda e: e.memset(bd64[:], 0.0), writes=[bf("bd64")])
        P.op("dve", lambda e: e.memset(bd64[0:64, 0:64], 1.0), writes=[bf("bd64")])
        P.op("dve", lambda e: e.memset(bd64[64:128, 64:128], 1.0), writes=[bf("bd64")])
        P.op("dve", lambda e: e.memset(S_sb[:], 0.0), writes=[bf("S")])
        P.op("dve", lambda e: e.memset(Sb_sb[:], 0.0), writes=[bf("Sb")])
        P.op("dve", lambda e: e.memset(qpe_sb[:], 0.0), writes=[bf("qpe")])
        P.dma("pool", tri[:], tri_d, writes=[bf("tri")])
        P.dma("pool", ident[:], ident_d, writes=[bf("ident")])
        P.dma("pool", wg_sb[64:80, :], wg, writes=[bf("wg")])
        for k in range(KD):
            P.dma("pool", gw_sb[:, k * 320:(k + 1) * 320], wmem[k * 128:(k + 1) * 128, :], writes=[bf("gw")])
        for k in range(2):
            P.dma("pool", wukv_sb[:, k, :], wukv[k * 128:(k + 1) * 128, :], writes=[bf("wukv")])
        for k in range(KD):
            P.dma("sp", stg[:, k * NW1:(k + 1) * NW1], w1[k * 128:(k + 1) * 128, :], writes=[bf(f"stg{k}")])
        for k in range(KD):
            if k % 2 == 0:
                P.op("dve", lambda e, k=k: e.tensor_copy(out=w1_sb[:, k, :], in_=stg[:, k * NW1:(k + 1) * NW1]),
                     reads=[bf(f"stg{k}")], writes=[bf(f"w1_{k}")])
            else:
                P.op("act", lambda e, k=k: e.activation(out=w1_sb[:, k, :], in_=stg[:, k * NW1:(k + 1) * NW1],
                                                        func=AF.Copy),
                     reads=[bf(f"stg{k}")], writes=[bf(f"w1_{k}")])
        for k in range(KD):
            P.dma("pool", w1t_sb[:, k, :], w1t[k * 128:(k + 1) * 128, :], writes=[bf("w1t")])

        def col(i, rows=128):
            return cols[0:rows, i:i + 1]

        def ncol(i, rows=128):
            return ncols[0:rows, i:i + 1]

        def mm(out_ap, out_buf, items, extra_reads=()):
            n = len(items)
            for j, (l, r, rd) in enumerate(items):
                P.op("pe", lambda e, l=l, r=r, j=j: e.matmul(out_ap, l, r, start=(j == 0), stop=(j == n - 1)),
                     reads=list(rd) + list(extra_reads), writes=[out_buf], signal=(j == n - 1))

        PSG = [ps]

        def rstd_begin(srcs, rows, N=TT):
            hs = []
            for ap, b in srcs:
                hq, hb = HP.get()
                P.op("act", lambda e, ap=ap, hq=hq: e.activation(out=hq[0:rows, 0:N], in_=ap, func=AF.Square),
                     reads=[b], writes=[hb])
                hs.append((hq, hb))
            return hs

        def rstd_end(hs, lhsT, lbuf, rows, dim, N=TT):
            pt, pb = PSG[0].get()
            n = len(hs)
            for j, (hq, hb) in enumerate(hs):
                P.op("pe", lambda e, hq=hq, j=j: e.matmul(pt[0:rows, 0:N], lhsT, hq[0:rows, 0:N],
                                                          start=(j == 0), stop=(j == n - 1)),
                     reads=[hb, lbuf], writes=[pb], signal=True)
            ft, fb = FP.get()
            P.op("act", lambda e: e.activation(out=ft[0:rows, 0:N], in_=pt[0:rows, 0:N], func=AF.Ln,
                                               scale=1.0 / dim, bias=EPS), reads=[pb], writes=[fb])
            P.op("act", lambda e: e.activation(out=ft[0:rows, 0:N], in_=ft[0:rows, 0:N], func=AF.Exp,
                                               scale=-0.5), reads=[fb], writes=[fb])
            return ft, fb

        def rstd_of(srcs, lhsT, lbuf, rows, dim, N=TT):
            pt, pb = PSG[0].get()
            n = len(srcs)
            for j, (ap, b) in enumerate(srcs):
                hq, hb = HP.get()
                P.op("act", lambda e, ap=ap, hq=hq: e.activation(out=hq[0:rows, 0:N], in_=ap, func=AF.Square),
                     reads=[b], writes=[hb])
                P.op("pe", lambda e, hq=hq, j=j: e.matmul(pt[0:rows, 0:N], lhsT, hq[0:rows, 0:N],
                                                          start=(j == 0), stop=(j == n - 1)),
                     reads=[hb, lbuf], writes=[pb], signal=True)
            ft, fb = FP.get()
            P.op("act", lambda e: e.activation(out=ft[0:rows, 0:N], in_=pt[0:rows, 0:N], func=AF.Ln,
                                               scale=1.0 / dim, bias=EPS), reads=[pb], writes=[fb])
            P.op("act", lambda e: e.activation(out=ft[0:rows, 0:N], in_=ft[0:rows, 0:N], func=AF.Exp,
                                               scale=-0.5), reads=[fb], writes=[fb])
            return ft, fb

        def sigmoid_to(out_ap, out_buf, in_ap, in_buf, nbias_ap, rows=128, N=TT):
            ft, fb = FP.get()
            if nbias_ap is None:
                P.op("act", lambda e: e.activation(out=ft[0:rows, 0:N], in_=in_ap, func=AF.Exp, scale=-1.0),
                     reads=[in_buf], writes=[fb])
            else:
                P.op("act", lambda e: e.activation(out=ft[0:rows, 0:N], in_=in_ap, func=AF.Exp, scale=-1.0,
                                                   bias=nbias_ap), reads=[in_buf, bf("ncols")], writes=[fb])
            P.op("act", lambda e: e.activation(out=ft[0:rows, 0:N], in_=ft[0:rows, 0:N], func=AF.Ln, bias=1.0),
                 reads=[fb], writes=[fb])
            P.op("act", lambda e: e.activation(out=out_ap, in_=ft[0:rows, 0:N], func=AF.Exp, scale=-1.0),
                 reads=[fb], writes=[out_buf])

        def recip_to(out_ap, out_buf, in_ap, in_buf, rows=128, N=TT):
            ft, fb = FP.get()
            P.op("act", lambda e: e.activation(out=ft[0:rows, 0:N], in_=in_ap, func=AF.Ln),
                 reads=[in_buf], writes=[fb])
            P.op("act", lambda e: e.activation(out=out_ap, in_=ft[0:rows, 0:N], func=AF.Exp, scale=-1.0),
                 reads=[fb], writes=[out_buf])

        def wchunk(k, i):
            o, w = _choff(i)
            return w1_sb[:, k, o:o + w], w

        def project(i, N=TT):
            o, w = _choff(i)
            pt, pb = PSG[0].get()
            mm(pt[0:w, 0:N], pb, [(w1_sb[:, k, o:o + w], h_sb[:, k, 0:N], [bf(f"w1_{k}"), bf(f"h{k}")])
                                  for k in range(KD)])
            return pt, pb

        ck(1)
        P.dma("sp", x_sb[:, :, 0:256], memT.rearrange("(k p) c -> p k c", p=128),
              writes=[bf(f"x{k}") for k in range(KD)])
        rm, rmb = rstd_of([(x_sb[:, k, 0:256], bf(f"x{k}")) for k in range(KD)], ones[:], bf("ones"),
                          128, float(D), N=256)
        for k in range(KD):
            P.op("dve", lambda e, k=k: e.scalar_tensor_tensor(
                out=h_sb[:, k, 0:256], in0=x_sb[:, k, 0:256], scalar=col(27 + k), in1=rm[:, 0:256],
                op0=ALU.mult, op1=ALU.mult), reads=[bf(f"x{k}"), bf("cols"), rmb], writes=[bf(f"h{k}")])
        pt, pb = ps.get()
        mm(pt[0:64, 0:256], pb, [(gw_sb[:, k * 320:k * 320 + 64], h_sb[:, k, 0:256], [bf("gw"), bf(f"h{k}")])
                                 for k in range(KD)])
        rk, rkb = rstd_of([(pt[0:64, 0:256], pb)], ones[0:64, 0:64], bf("ones"), 64, 64.0, N=256)
        P.op("dve", lambda e: e.scalar_tensor_tensor(
            out=kmT_sb[0:64, :], in0=pt[0:64, 0:256], scalar=col(17, 64), in1=rk[0:64, 0:256],
            op0=ALU.mult, op1=ALU.mult), reads=[pb, bf("cols"), rkb], writes=[bf("kmT")])
        for mb in range(2):
            pt, pb = ps.get()
            mm(pt[:, 0:256], pb, [(h_sb[:, k, mb * 128:(mb + 1) * 128], gw_sb[:, k * 320 + 64:(k + 1) * 320],
                                   [bf("gw"), bf(f"h{k}")]) for k in range(KD)])
            P.op("act", lambda e, mb=mb, pt=pt: e.activation(out=vm_sb[:, mb, :], in_=pt[:, 0:256], func=AF.Copy),
                 reads=[pb], writes=[bf("vm")])

        ck(2)
        TWO_PI = 2.0 * np.pi
        C1 = float(np.float32(6.28125))
        C2 = float(TWO_PI - 6.28125)
        PI_C = 3.1415925
        ALLB = [0, 1, 2, 3, 5]

        def gtile(j):
            return gw_sb[:, j * TT:(j + 1) * TT]

        def load_x(T):
            t0 = T * TT
            P.dma("sp", x_sb[:], xT[:, t0:t0 + TT].rearrange("(k p) c -> p k c", p=128),
                  writes=[bf(f"x{k}") for k in range(KD)])

        def normed(pt, pb, rt, rtb, gcol, out_ap, out_buf, rows=128):
            P.op("dve", lambda e: e.scalar_tensor_tensor(
                out=out_ap, in0=pt[0:rows, 0:TT], scalar=gcol, in1=rt[0:rows, 0:TT],
                op0=ALU.mult, op1=ALU.mult), reads=[pb, rtb, bf("cols")], writes=[out_buf])

        def tables_dve(T):
            t0 = T * TT
            sni = sn_sb[:].bitcast(I32)
            P.dma("sp", sni, pos[0, t0:t0 + TT].partition_broadcast(128), writes=[bf("sn")])
            P.op("dve", lambda e: e.tensor_copy(out=cs_sb[:], in_=sni), reads=[bf("sn")], writes=[bf("cs")])
            P.op("dve", lambda e: e.tensor_scalar(out=cs_sb[:], in0=cs_sb[:], scalar1=col(35), scalar2=None,
                                                  op0=ALU.mult), reads=[bf("cs"), bf("cols")], writes=[bf("cs")])
            P.op("dve", lambda e: e.tensor_scalar(out=sni, in0=cs_sb[:], scalar1=1.0 / TWO_PI, scalar2=None,
                                                  op0=ALU.mult), reads=[bf("cs")], writes=[bf("sn")])
            P.op("dve", lambda e: e.tensor_copy(out=sn_sb[:], in_=sni), reads=[bf("sn")], writes=[bf("sn")])
            for cc in (C1, C2):
                P.op("dve", lambda e, cc=cc: e.scalar_tensor_tensor(
                    out=cs_sb[:], in0=sn_sb[:], scalar=-cc, in1=cs_sb[:], op0=ALU.mult, op1=ALU.add),
                    reads=[bf("sn"), bf("cs")], writes=[bf("cs")])
            P.op("dve", lambda e: e.tensor_scalar(out=sn_sb[:], in0=cs_sb[:], scalar1=float(np.pi),
                                                  scalar2=-TWO_PI, op0=ALU.is_gt, op1=ALU.mult),
                 reads=[bf("cs")], writes=[bf("sn")])
            P.op("dve", lambda e: e.tensor_tensor(out=cs_sb[:], in0=cs_sb[:], in1=sn_sb[:], op=ALU.add),
                 reads=[bf("cs"), bf("sn")], writes=[bf("cs")])
            P.op("dve", lambda e: e.tensor_scalar(out=cs_sb[:], in0=cs_sb[:], scalar1=PI_C, scalar2=-PI_C,
                                                  op0=ALU.min, op1=ALU.max), reads=[bf("cs")], writes=[bf("cs")])

        def tables_act(T):
            P.op("act", lambda e: e.activation(out=sn_sb[:], in_=cs_sb[:], func=AF.Sin),
                 reads=[bf("cs")], writes=[bf("sn")])
            P.op("act", lambda e: e.activation(out=cs_sb[:], in_=cs_sb[:], func=AF.Abs),
                 reads=[bf("cs")], writes=[bf("cs")])
            P.op("act", lambda e: e.activation(out=cs_sb[:], in_=cs_sb[:], func=AF.Sin, scale=-1.0,
                                               bias=ncol(37)), reads=[bf("cs"), bf("ncols")], writes=[bf("cs")])
            P.op("dve", lambda e: e.tensor_scalar(out=sn_sb[:], in0=sn_sb[:], scalar1=col(36), scalar2=None,
                                                  op0=ALU.mult), reads=[bf("sn"), bf("cols")], writes=[bf("sn")])

        xn = {}

        def xnorm_a(T):
            xn["r"] = rstd_of([(x_sb[:, k, :], bf(f"x{k}")) for k in range(KD)], ones[:], bf("ones"),
                              128, float(D))

        def xnorm_b(T):
            rx, rxb = xn["r"]
            for k in range(KD):
                P.op("dve", lambda e, k=k: e.scalar_tensor_tensor(
                    out=h_sb[:, k, :], in0=x_sb[:, k, :], scalar=col(k), in1=rx[:],
                    op0=ALU.mult, op1=ALU.mult), reads=[bf(f"x{k}"), bf("cols"), rxb], writes=[bf(f"h{k}")])
            if T + 1 < ntile:
                load_x(T + 1)

        def kv1(T):
            p4, p4b = project(4)
            h4 = rstd_begin([(p4[:, 0:TT], p4b)], 128)
            p5, p5b = project(5)
            h5 = rstd_begin([(p5[:, 0:TT], p5b)], 128)
            rc, rcb = rstd_end(h4 + h5, ones[:], bf("ones"), 128, 256.0)
            normed(p4, p4b, rc, rcb, col(14), ckvn_sb[:, 0, :], bf("ckvn"))
            normed(p5, p5b, rc, rcb, col(15), ckvn_sb[:, 1, :], bf("ckvn"))

        def kv_k(T):
            t0 = T * TT
            pp, hh = [], []
            for h in range(2):
                pt, pb = PSG[0].get()
                mm(pt[:, 0:TT], pb, [(wukv_sb[:, k, h * 128:(h + 1) * 128], ckvn_sb[:, k, :],
                                      [bf("wukv"), bf("ckvn")]) for k in range(2)])
                pp.append((pt, pb))
                hh.append(rstd_begin([(pt[:, 0:TT], pb)], 128))
            for h in range(2):
                pt, pb = pp[h]
                rt, rtb = rstd_end(hh[h], ones[:], bf("ones"), 128, 128.0)
                normed(pt, pb, rt, rtb, col(9), kn_c[:, h, t0:t0 + TT], bf(f"kn{h}_{T}"))

        def kv_v(T):
            for blk in range(4):
                pt, pb = PSG[0].get()
                mm(pt[:, 0:256], pb, [(ckvn_sb[:, k, blk * 128:(blk + 1) * 128], wukv_sb[:, k, 256:512],
                                       [bf("wukv"), bf("ckvn")]) for k in range(2)])
                P.op("act", lambda e, blk=blk, pt=pt: e.activation(out=v_c[:, T * 4 + blk, :], in_=pt[:, 0:256],
                                                                  func=AF.Copy),
                     reads=[pb], writes=[bf(f"v{T * 4 + blk}")])

        def rope_finish(pa, pab, pb_, pbb, rr, rrb, ga, gb, out_ap, out_buf):
            fa, fab = FP.get()
            normed(pa, pab, rr, rrb, col(ga), fa[:], fab)
            P.op("dve", lambda e: e.tensor_tensor(out=fa[:], in0=fa[:], in1=cs_sb[:], op=ALU.mult),
                 reads=[fab, bf("cs")], writes=[fab])
            fb_, fbb = FP.get()
            normed(pb_, pbb, rr, rrb, col(gb), fb_[:], fbb)
            P.op("dve", lambda e: e.tensor_tensor(out=fb_[:], in0=fb_[:], in1=sn_sb[:], op=ALU.mult),
                 reads=[fbb, bf("sn")], writes=[fbb])
            if out_ap is None:
                for h in range(2):
                    rs_ = slice(64 * h, 64 * h + 64)
                    P.op("dve", lambda e, h=h, rs_=rs_: e.tensor_tensor(out=qpe_sb[rs_, h, :], in0=fa[rs_, :],
                                                                      in1=fb_[rs_, :], op=ALU.add),
                         reads=[fab, fbb], writes=[out_buf])
            else:
                P.op("dve", lambda e: e.tensor_tensor(out=out_ap, in0=fa[:], in1=fb_[:], op=ALU.add),
                     reads=[fab, fbb], writes=[out_buf])

        def kv_rope(T):
            t0 = T * TT
            pa, pab = project(6)
            ha = rstd_begin([(pa[:, 0:TT], pab)], 128)
            pb_, pbb = project(7)
            rr, rrb = rstd_end(ha, bd64[:], bf("bd64"), 128, 64.0)
            rope_finish(pa, pab, pb_, pbb, rr, rrb, 12, 13, kpe_c[:, t0:t0 + TT], bf(f"kpe_{T}"))

        def next_tile_stages(T):
            return [lambda: tables_dve(T), lambda: xnorm_a(T), lambda: xnorm_b(T), lambda: tables_act(T),
                    lambda: kv1(T), lambda: kv_k(T), lambda: kv_v(T), lambda: kv_rope(T)]

        def pro_q(T):
            p0, p0b = project(0)
            h0 = rstd_begin([(p0[:, 0:TT], p0b)], 128)
            p1, p1b = project(1)
            h1 = rstd_begin([(p1[:, 0:TT], p1b)], 128)
            r0, r0b = rstd_end(h0, ones[:], bf("ones"), 128, 128.0)
            normed(p0, p0b, r0, r0b, col(8), qn_sb[:, 0, :], bf("qn0"))
            p2, p2b = project(2)
            h2 = rstd_begin([(p2[:, 0:TT], p2b)], 128)
            r1, r1b = rstd_end(h1, ones[:], bf("ones"), 128, 128.0)
            normed(p1, p1b, r1, r1b, col(8), qn_sb[:, 1, :], bf("qn1"))
            p3, p3b = project(3)
            rr, rrb = rstd_end(h2, bd64[:], bf("bd64"), 128, 64.0)
            rope_finish(p2, p2b, p3, p3b, rr, rrb, 10, 11, None, bf("qpe"))

        def pro_rest(T):
            ck(6)
            pq, pqb = project(12)
            hq_ = rstd_begin([(pq[0:64, 0:TT], pqb)], 64)
            P.op("act", lambda e: e.activation(out=qa_sb[64:80, :], in_=pq[64:80, 0:TT], func=AF.Copy),
                 reads=[pqb], writes=[bf("ag")])
            p8, p8b = project(8)
            i8 = ps.last
            rq, rqb = rstd_end(hq_, ones[0:64, 0:64], bf("ones"), 64, 64.0)
            P.op("dve", lambda e: e.scalar_tensor_tensor(
                out=qa_sb[0:64, :], in0=pq[0:64, 0:TT], scalar=col(16, 64), in1=rq[0:64, :],
                op0=ALU.mult, op1=ALU.mult), reads=[pqb, rqb, bf("cols")], writes=[bf("qm")])
            p9, p9b = project(9)
            i9 = ps.last
            ps.active = [i for i in ALLB if i not in (i8, i9)]
            pz, pzb = ps.get()
            mm(pz[:, 0:TT], pzb, [(wg_sb[64:80, :], qa_sb[64:80, :], [bf("wg"), bf("ag")])])
            sp_, spb = FP.get()
            P.op("act", lambda e: e.activation(out=sp_[:], in_=pz[:, 0:TT], func=AF.Exp, scale=-1.0,
                                               bias=ncol(24)), reads=[pzb, bf("ncols")], writes=[spb])
            P.op("act", lambda e: e.activation(out=sp_[:], in_=sp_[:], func=AF.Ln, bias=1.0),
                 reads=[spb], writes=[spb])
            for n in range(4):
                P.op("dve", lambda e, n=n: e.tensor_tensor_scan(
                    out=cum_sb[:, n * 128:(n + 1) * 128], data0=onesf[:], data1=sp_[:, n * 128:(n + 1) * 128],
                    initial=0.0, op0=ALU.mult, op1=ALU.add), reads=[spb, bf("onesf")], writes=[bf("cum")])
            for blk in range(4):
                pt, pb = ps.get()
                mm(pt[:, 0:256], pb, [(h_sb[:, k, blk * 128:(blk + 1) * 128], w1t_sb[:, k, :],
                                       [bf("w1t"), bf(f"h{k}")]) for k in range(KD)])
                P.op("act", lambda e, blk=blk, pt=pt: e.activation(out=vg_sb[:, blk, :], in_=pt[:, 0:256],
                                                                  func=AF.Copy),
                     reads=[pb], writes=[bf(f"vg{blk}")])
            epos, eposb = FP.get()
            eneg, enegb = FP.get()
            P.op("act", lambda e: e.activation(out=epos[:], in_=cum_sb[:], func=AF.Exp, scale=1.0 / 16),
                 reads=[bf("cum")], writes=[eposb])
            P.op("act", lambda e: e.activation(out=eneg[:], in_=cum_sb[:], func=AF.Exp, scale=-1.0 / 16),
                 reads=[bf("cum")], writes=[enegb])
            P.op("dve", lambda e: e.scalar_tensor_tensor(
                out=qg_sb[:], in0=p8[:, 0:TT], scalar=SC_GLA, in1=eneg[:], op0=ALU.mult, op1=ALU.mult),
                reads=[p8b, enegb], writes=[bf("qg")])
            P.op("dve", lambda e: e.tensor_tensor(out=kg_sb[:], in0=p9[:, 0:TT], in1=epos[:], op=ALU.mult),
                 reads=[p9b, eposb], writes=[bf("kg")])
            P.op("act", lambda e: e.activation(out=kgf_sb[:], in_=p9[:, 0:TT], func=AF.Copy),
                 reads=[p9b], writes=[bf("kgf")])
            ps.active = ALLB

        def gate_stage(j):
            if j < 6:
                pt, pb = project(13 + j)
                sigmoid_to(gtile(j), bf(f"g{j}"), pt[:, 0:TT], pb, ncol(18 + j))
            else:
                pt, pb = project(10 + j - 6)
                sg, sgb = FP.get()
                sigmoid_to(sg[:], sgb, pt[:, 0:TT], pb, None)
                P.op("dve", lambda e: e.tensor_tensor(out=gtile(j), in0=pt[:, 0:TT], in1=sg[:], op=ALU.mult),
                     reads=[pb, sgb], writes=[bf(f"g{j}")])

        def gate_stages():
            return [(lambda j=j: gate_stage(j)) for j in range(8)]

        ps2 = PsumPool.__new__(PsumPool)
        ps2.t, ps2.b, ps2.i, ps2.active = ps.t, ps.b, 0, [4, 6, 2]

        def run_stage(f, h):
            ps2.active = [4, 6, 2] if h == 0 else [3, 5, 2]
            PSG[0] = ps2
            f()
            PSG[0] = ps

        def attention(T, h, stages):
            nkb = 4 * T + 4
            ob, lb = b_o[h], b_l[h]
            ot, lt = o_ps[h], l_ps[h]
            pend = []
            stages = list(stages)
            gap = max(1, nkb // (len(stages) + 1)) if stages else 1

            def score(kb):
                jd = kb - 4 * T
                c0 = 128 * jd if jd > 0 else 0
                pt, pb = ps.get()
                mm(pt[:, c0:TT], pb, [
                    (kn_c[:, h, kb * 128:(kb + 1) * 128], qn_sb[:, h, c0:TT],
                     [bf(f"kn{h}_{kb // 4}"), bf(f"qn{h}")]),
                    (kpe_c[:, kb * 128:(kb + 1) * 128], qpe_sb[:, h, c0:TT],
                     [bf(f"kpe_{kb // 4}"), bf("qpe")])])
                pT, pTb = PT.get()
                P.op("act", lambda e: e.activation(out=pT[:, c0:TT], in_=pt[:, c0:TT], func=AF.Exp,
                                                   scale=SC_MLA), reads=[pb], writes=[pTb])
                if jd >= 0:
                    P.op("pool", lambda e: e.tensor_tensor(out=pT[:, c0:c0 + 128], in0=pT[:, c0:c0 + 128],
                                                           in1=tri[:], op=ALU.mult),
                         reads=[pTb, bf("tri")], writes=[pTb])
                return (kb, c0, pT, pTb)

            def pv(item):
                kb, c0, pT, pTb = item
                first, last = (kb == 0), (kb == nkb - 1)
                P.op("pe", lambda e: e.matmul(ot[:, c0:TT], v_c[:, kb, h * 128:(h + 1) * 128], pT[:, c0:TT],
                                              start=first, stop=last),
                     reads=[bf(f"v{kb}"), pTb], writes=[ob], signal=last)
                P.op("pe", lambda e: e.matmul(lt[:, c0:TT], ones[:], pT[:, c0:TT], start=first, stop=last),
                     reads=[bf("ones"), pTb], writes=[lb], signal=True)

            LOOK = 3
            for kb in range(nkb):
                pend.append(score(kb))
                if len(pend) > LOOK:
                    pv(pend.pop(0))
                if stages and kb % gap == gap - 1:
                    run_stage(stages.pop(0), h)
            while pend:
                pv(pend.pop(0))
            while stages:
                run_stage(stages.pop(0), h)
            rl, rlb = FP.get()
            recip_to(rl[:], rlb, lt[:, 0:TT], lb)
            P.op("dve", lambda e: e.tensor_tensor(out=omla_sb[:, h, :], in0=ot[:, 0:TT], in1=rl[:], op=ALU.mult),
                 reads=[ob, rlb], writes=[bf(f"omla{h}")])
            P.op("pool", lambda e: e.tensor_tensor(out=omla_sb[:, h, :], in0=omla_sb[:, h, :], in1=gtile(h),
                                                   op=ALU.mult),
                 reads=[bf(f"omla{h}"), bf(f"g{h}")], writes=[bf(f"omla{h}")])
            while stages:
                run_stage(stages.pop(0), h)

        def gla_stages(T):
            out = []
            for n in range(4):
                st_ = {}
                cs_ = slice(n * 128, (n + 1) * 128)
                nb = small[:, 2 * n:2 * n + 1]
                dc = small[:, 2 * n + 1:2 * n + 2]

                def s1(n=n, st_=st_, cs_=cs_, nb=nb, dc=dc):
                    P.op("dve", lambda e: e.tensor_scalar(
                        out=nb, in0=cum_sb[:, n * 128 + 127:n * 128 + 128], scalar1=-1.0 / 16, scalar2=None,
                        op0=ALU.mult), reads=[bf("cum")], writes=[bf(f"nb{n}")])
                    P.op("act", lambda e: e.activation(out=dc, in_=nb, func=AF.Exp),
                         reads=[bf(f"nb{n}")], writes=[bf(f"dc{n}")])
                    kd, kdb = FP.get()
                    P.op("act", lambda e: e.activation(out=kd[:, 0:128], in_=cum_sb[:, cs_], func=AF.Exp,
                                                       scale=1.0 / 16, bias=nb),
                         reads=[bf("cum"), bf(f"nb{n}")], writes=[kdb])
                    kdT, kdTb = HP.get()
                    P.op("dve", lambda e: e.tensor_tensor(out=kdT[:, 0:128], in0=kgf_sb[:, cs_], in1=kd[:, 0:128],
                                                          op=ALU.mult), reads=[bf("kgf"), kdb], writes=[kdTb])
                    st_["kdT"] = (kdT, kdTb)

                def s2(n=n, st_=st_, cs_=cs_):
                    kdT, kdTb = st_["kdT"]
                    P.op("pe", lambda e: e.transpose(tp_ps[:, 0:128], kdT[:, 0:128], ident[:]),
                         reads=[kdTb, bf("ident")], writes=[b_tp])
                    kdk, kdkb = HP.get()
                    P.op("act", lambda e: e.activation(out=kdk[:, 0:128], in_=tp_ps[:, 0:128], func=AF.Copy),
                         reads=[b_tp], writes=[kdkb])
                    pa, pab = PSG[0].get()
                    mm(pa[:, 0:128], pab, [(kg_sb[:, cs_], qg_sb[:, cs_], [bf("kg"), bf("qg")])])
                    am, amb = HP.get()
                    P.op("dve", lambda e: e.tensor_tensor(out=am[:, 0:128], in0=pa[:, 0:128], in1=tri[:],
                                                          op=ALU.mult), reads=[pab, bf("tri")], writes=[amb])
                    st_["kdk"] = (kdk, kdkb)
                    st_["am"] = (am, amb)

                def s3(n=n, st_=st_, cs_=cs_, dc=dc):
                    kdk, kdkb = st_["kdk"]
                    am, amb = st_["am"]
                    po, pob = PSG[0].get()
                    for eh in range(2):
                        mm(po[:, eh * 128:(eh + 1) * 128], pob, [
                            (vg_sb[:, n, eh * 128:(eh + 1) * 128], am[:, 0:128], [bf(f"vg{n}"), amb]),
                            (Sb_sb[:, eh * 128:(eh + 1) * 128], qg_sb[:, cs_], [bf("Sb"), bf("qg")])])
                    for eh in range(2):
                        P.op("act", lambda e, eh=eh: e.activation(
                            out=og_sb[:, eh, cs_], in_=po[:, eh * 128:(eh + 1) * 128], func=AF.Copy),
                            reads=[pob], writes=[bf(f"og{eh}")])
                    pd, pdb = PSG[0].get()
                    mm(pd[:, 0:256], pdb, [(kdk[:, 0:128], vg_sb[:, n, :], [kdkb, bf(f"vg{n}")])])
                    P.op("dve", lambda e: e.scalar_tensor_tensor(
                        out=S_sb[:], in0=S_sb[:], scalar=dc, in1=pd[:, 0:256], op0=ALU.mult, op1=ALU.add),
                        reads=[bf("S"), bf(f"dc{n}"), pdb], writes=[bf("S")])
                    P.op("act", lambda e: e.activation(out=Sb_sb[:], in_=S_sb[:], func=AF.Copy),
                         reads=[bf("S")], writes=[bf("Sb")])

                out += [s1, s2, s3]
            return out

        fin = {}

        def fin_a(T):
            rg, rgb = rstd_of([(og_sb[:, eh, :], bf(f"og{eh}")) for eh in range(2)], ones[:], bf("ones"),
                              128, 256.0)
            for eh in range(2):
                P.op("dve", lambda e, eh=eh: e.scalar_tensor_tensor(
                    out=og_sb[:, eh, :], in0=og_sb[:, eh, :], scalar=col(25 + eh), in1=rg[:],
                    op0=ALU.mult, op1=ALU.mult), reads=[bf(f"og{eh}"), rgb, bf("cols")], writes=[bf(f"og{eh}")])
                for j in (6 + eh, 2 + eh):
                    P.op("pool", lambda e, eh=eh, j=j: e.tensor_tensor(
                        out=og_sb[:, eh, :], in0=og_sb[:, eh, :], in1=gtile(j), op=ALU.mult),
                        reads=[bf(f"og{eh}"), bf(f"g{j}")], writes=[bf(f"og{eh}")])
            ck(9)
            pts = []
            for mb in range(2):
                pt, pb = ps.get()
                mm(pt[:, 0:TT], pb, [(kmT_sb[0:64, mb * 128:(mb + 1) * 128], qa_sb[0:64, :],
                                      [bf("kmT"), bf("qm")])])
                pT, pTb = PT.get()
                P.op("act", lambda e, pt=pt, pT=pT: e.activation(out=pT[:], in_=pt[:, 0:TT], func=AF.Exp,
                                                                 scale=SC_MEM), reads=[pb], writes=[pTb])
                pts.append((pT, pTb))
            fin["pts"] = pts

        def fin_b(T):
            t0 = T * TT
            pts = fin["pts"]
            pm_os = [(ps.t[4], ps.b[4]), (ps.t[6], ps.b[6])]
            pm_l, pm_lb = ps.get()
            for mb in range(2):
                pT, pTb = pts[mb]
                P.op("pe", lambda e, pT=pT, mb=mb: e.matmul(pm_l[:, 0:TT], ones[:], pT[:], start=(mb == 0),
                                                            stop=(mb == 1)),
                     reads=[bf("ones"), pTb], writes=[pm_lb], signal=True)
            rl, rlb = FP.get()
            recip_to(rl[:], rlb, pm_l[:, 0:TT], pm_lb)
            for eh in range(2):
                pm_o, pm_ob = pm_os[eh]
                for mb in range(2):
                    pT, pTb = pts[mb]
                    P.op("pe", lambda e, pT=pT, mb=mb, eh=eh, pm_o=pm_o: e.matmul(
                        pm_o[:, 0:TT], vm_sb[:, mb, eh * 128:(eh + 1) * 128], pT[:], start=(mb == 0),
                        stop=(mb == 1)), reads=[bf("vm"), pTb], writes=[pm_ob], signal=(mb == 1))
                om, omb = FP.get()
                P.op("dve", lambda e, om=om, pm_o=pm_o: e.tensor_tensor(out=om[:], in0=pm_o[:, 0:TT], in1=rl[:],
                                                                        op=ALU.mult),
                     reads=[pm_ob, rlb], writes=[omb])
                P.op("dve", lambda e, om=om, eh=eh: e.tensor_tensor(out=om[:], in0=om[:], in1=gtile(4 + eh),
                                                                   op=ALU.mult),
                     reads=[omb, bf(f"g{4 + eh}")], writes=[omb])
                ck(10)
                P.op("dve", lambda e, om=om, eh=eh: e.tensor_tensor(out=omla_sb[:, eh, :], in0=omla_sb[:, eh, :],
                                                                   in1=om[:], op=ALU.add),
                     reads=[bf(f"omla{eh}"), omb], writes=[bf(f"omla{eh}")])
                P.op("dve", lambda e, eh=eh: e.tensor_tensor(out=omla_sb[:, eh, :], in0=omla_sb[:, eh, :],
                                                             in1=og_sb[:, eh, :], op=ALU.add),
                     reads=[bf(f"omla{eh}"), bf(f"og{eh}")], writes=[bf(f"omla{eh}")])
            if fused:
                for _ in range(4 if T < 14 else 0):
                    o_, i_, nm = precast.pop(0)
                    P.dma("pool", o_, i_, writes=[bf(nm)])
                qq, qo = T // 4, (T % 4) * TT
                P.dma("pool", ybuf[qq][:, qo:qo + TT].rearrange("(j p) c -> p j c", p=128), omla_sb[:],
                      reads=[bf("omla0"), bf("omla1")], writes=[bf(f"ybuf{qq}")])
                if T % 4 == 3:
                    P.collective("AllGather", [ybuf[qq]], [ygath[qq * D:(qq + 1) * D, :]],
                                 [[0, 1, 2, 3], [4, 5, 6, 7]],
                                 reads=[bf(f"ybuf{qq}")], writes=[bf(f"ygath{qq}")])
            else:
                P.dma("sp", yT[:, t0:t0 + TT].rearrange("(j p) c -> p j c", p=128), omla_sb[:],
                      reads=[bf("omla0"), bf("omla1")], is_store=True)

        load_x(0)
        for f in next_tile_stages(0):
            f()
        pro_q(0)
        for T in range(ntile):
            ck(3)
            ps.active = ALLB
            pro_rest(T)
            ck(7)
            ps.active = [0, 1, 2]
            attention(T, 0, gate_stages() + (next_tile_stages(T + 1) if T + 1 < ntile else []))
            attention(T, 1, gla_stages(T))
            ck(8)
            ps.active = ALLB
            fin_a(T)
            if T + 1 < ntile:
                pro_q(T + 1)
            fin_b(T)
        if not fused:
            P.finish()
    if fused:
        P.barrier()
        with ExitStack() as st2:
            rank = nc.sync.partition_id() % 4
            ygq = [ygath[q * D:(q + 1) * D, :].rearrange("(k p) c -> p k c", p=128) for q in range(4)]
            y_src = lambda t: ygq[t // 2][:, :, bass.ds(rank * 512 + (t % 2) * TT2, TT2)]
            while precast:
                o_, i_, nm = precast.pop(0)
                P.dma("pool", o_, i_, writes=[bf(nm)])
            emit_phase_c(nc, P, st2, y_src, "sp", xq, w_o, w_up, w_dn, g_ffn, oT, y_reads=lambda t: [bf(f"ygath{t // 2}")],
                         pre=(wo_b, wu_b, wd_b, bf("wo_b"), bf("wu_b"), bf("wd_b")))
            P.finish()
    st0.close()
    return nc


def _sw(a):
    n = a.shape[-1] // 2
    return np.concatenate([a[..., n:], a[..., :n]], axis=-1)


def _tile128(v):
    return np.ascontiguousarray(v.reshape(-1, 128).T)


def prep_l1(inp, b, g):
    f32 = np.float32
    w_in = inp["w_in"][0]
    h0, h1 = 2 * g, 2 * g + 1
    qn = lambda h: w_in[:, 192 * h:192 * h + 128]
    qr = lambda h: w_in[:, 192 * h + 128:192 * h + 192]
    ckv = w_in[:, 1536:1792]
    kr = w_in[:, 1792:1856]
    qg = w_in[:, 1856 + 128 * g:1856 + 128 * g + 128]
    kg = w_in[:, 2368 + 128 * g:2368 + 128 * g + 128]
    vg = w_in[:, 2880 + 256 * g:2880 + 256 * g + 256]
    ag = w_in[:, 3904:3920]
    rg = w_in[:, 3920 + 256 * g:3920 + 256 * g + 256]
    qm = w_in[:, 4944 + 64 * g:4944 + 64 * g + 64]
    gate = lambda k: w_in[:, 5200 + 1024 * k + 256 * g:5200 + 1024 * k + 256 * g + 256]
    chunks = [qn(h0), qn(h1),
              np.concatenate([qr(h0), qr(h1)], 1), np.concatenate([_sw(qr(h0)), _sw(qr(h1))], 1),
              ckv[:, 0:128], ckv[:, 128:256],
              np.concatenate([kr, kr], 1), np.concatenate([_sw(kr), _sw(kr)], 1),
              qg, kg, rg[:, 0:128], rg[:, 128:256],
              np.concatenate([qm, ag], 1),
              gate(0), gate(1), gate(2)]
    w1 = np.ascontiguousarray(np.concatenate(chunks, 1), dtype=f32)
    assert w1.shape == (D, NW1), w1.shape
    w_ukv = inp["w_ukv"][0]
    kc = lambda h: w_ukv[:, 256 * h:256 * h + 128]
    vc = lambda h: w_ukv[:, 256 * h + 128:256 * h + 256]
    wukv = np.ascontiguousarray(np.concatenate([kc(h0), kc(h1), vc(h0), vc(h1)], 1), dtype=f32)
    wgm = np.ascontiguousarray(inp["w_gla_gate"][0][:, 128 * g:128 * g + 128], dtype=f32)
    wm = inp["w_mem_kv"][0]
    wmem = np.ascontiguousarray(np.concatenate([wm[:, 64 * g:64 * g + 64],
                                                wm[:, 256 + 256 * g:256 + 256 * g + 256]], 1), dtype=f32)
    cols = np.zeros((128, NCOL), f32)
    cols[:, 0:8] = _tile128(inp["g_mix"][0])
    cols[:, 8] = inp["g_q_nope"][0]
    cols[:, 9] = inp["g_k_nope"][0]
    gq, gk = inp["g_q_rope"][0], inp["g_k_rope"][0]
    cols[:, 10] = np.concatenate([gq, gq])
    cols[:, 11] = np.concatenate([_sw(gq), _sw(gq)])
    cols[:, 12] = np.concatenate([gk, gk])
    cols[:, 13] = np.concatenate([_sw(gk), _sw(gk)])
    cols[:, 14:16] = _tile128(inp["g_ckv"][0])
    cols[:, 16] = np.concatenate([inp["g_q_mem"][0]] * 2)
    cols[:, 17] = np.concatenate([inp["g_k_mem"][0]] * 2)
    bg = inp["b_gate"][0]
    for k in range(3):
        for j in range(2):
            o = 1024 * k + 256 * g + 128 * j
            cols[:, 18 + 2 * k + j] = bg[o:o + 128]
    cols[:, 24] = inp["b_gla_gate"][0][128 * g:128 * g + 128]
    cols[:, 25:27] = _tile128(inp["g_gla_out"][0])
    cols[:, 27:35] = _tile128(inp["g_mem"][0])
    inv = (1.0 / (10000.0 ** (np.arange(0, 64, 2, dtype=f32) / f32(64)))).astype(f32)
    cols[:, 35] = np.tile(inv, 4)
    cols[:, 36] = np.tile(np.concatenate([-np.ones(32, f32), np.ones(32, f32)]), 2)
    cols[:, 37] = f32(-np.pi / 2)
    tri = np.triu(np.ones((128, 128), f32))
    ident = np.eye(128, dtype=f32)
    return {
        "xT": np.ascontiguousarray(inp["x"][b].T),
        "pos": np.ascontiguousarray(inp["positions"][b:b + 1]).astype(np.int32),
        "w1": w1, "w1t": np.ascontiguousarray(vg, dtype=f32), "wukv": wukv, "wg": wgm,
        "memT": np.ascontiguousarray(inp["mem"][b].T), "wmem": wmem,
        "cols": cols, "tri": tri, "ident": ident,
    }


def kernel(**inp):
    inp = {k: np.asarray(v) for k, v in inp.items()}
    nc = build_l1(fused=True)
    w_o = np.ascontiguousarray(inp["w_o"][0])
    w_up = np.ascontiguousarray(inp["w_up"][0])
    w_dn = np.ascontiguousarray(inp["w_down"][0])
    g = _tile128(inp["g_ffn"][0])
    maps = []
    for c in range(NCORES):
        b, q = c // 4, c % 4
        m = prep_l1(inp, b, q)
        tok = np.concatenate([np.arange(Q * TOK2 + q * 512, Q * TOK2 + q * 512 + 512) for Q in range(4)])
        m["xq"] = np.ascontiguousarray(m["xT"][:, tok])
        m.update({"w_o": w_o, "w_up": w_up, "w_down": w_dn, "g_ffn": g})
        maps.append(m)
    res = run_bass_kernel_spmd(nc, maps, core_ids=list(range(NCORES)))
    out = np.empty((B, S, D), np.float32)
    for c in range(NCORES):
        b, q = c // 4, c % 4
        tok = np.concatenate([np.arange(Q * TOK2 + q * 512, Q * TOK2 + q * 512 + 512) for Q in range(4)])
        out[b, tok] = res.results[c]["oT"].T
    return out
```

```python
from contextlib import ExitStack

import numpy as np
import ml_dtypes

import concourse.bass as bass
import concourse.mybir as mybir
from concourse.bass_utils import run_bass_kernel_spmd

F32 = mybir.dt.float32
BF16 = mybir.dt.bfloat16
I32 = mybir.dt.int32
AF = mybir.ActivationFunctionType
ALU = mybir.AluOpType

D = 1024
B = 2
S = 8192
DFF = 4096
EPS = 1e-6
NCORES = 8


class Buf:
    __slots__ = ("w", "r", "name", "excl")

    def __init__(self, name="", excl=False):
        self.w = []
        self.r = {}
        self.name = name
        self.excl = excl


class Prog:
    LIMIT = 6000
    NDMA = 6
    EMBED = False

    def __init__(self, nc, stack):
        self.nc = nc
        self.stack = stack
        self.nsem = 0
        self.E = {}
        for name, h in (("pe", nc.tensor), ("act", nc.scalar), ("dve", nc.vector),
                        ("pool", nc.gpsimd), ("sp", nc.sync)):
            self.E[name] = {"h": h, "sem": None, "cnt": 0, "seen": {}, "key": None,
                            "slots": None, "rr": 0}
        self.stores = []
        self.stopped = False
        self.extra_toks = []

    def new_sem(self, name):
        self.nsem += 1
        s = self.stack.enter_context(self.nc.semaphore(f"{name}_{self.nsem}"))
        return (f"{name}_{self.nsem}", s)

    def _wait(self, eng, deps, defer_last=False):
        e = self.E[eng]
        best = {}
        for t in deps:
            if t is None:
                continue
            key, sem, val, src = t
            if eng == "pe" and src == "pe":
                continue
            if e["seen"].get(key, 0) >= val:
                continue
            if key not in best or best[key][1] < val:
                best[key] = (sem, val)
        items = list(best.items())
        last = None
        if defer_last and items:
            last = items.pop()
        for key, (sem, val) in items:
            e["h"].wait_ge(sem, val)
            e["seen"][key] = val
        if last is not None:
            e["seen"][last[0]] = last[1][1]
            return last[1]
        return None

    def _deps(self, reads, writes):
        deps = []
        for b in reads:
            deps.extend(b.w)
            if b.excl:
                deps.extend(b.r.values())
        for b in writes:
            deps.extend(b.w)
            deps.extend(b.r.values())
        return deps

    def _mark(self, tok, rkey, reads, writes):
        for b in reads:
            if b.excl:
                b.w = [tok]
                b.r = {}
            else:
                b.r[rkey] = tok
        for b in writes:
            if tok[3] == "dma" and not b.r and all(t[3] == "dma" for t in b.w):
                b.w = b.w + [tok]
            else:
                b.w = [tok]
            b.r = {}

    def op(self, eng, fn, reads=(), writes=(), signal=True):
        if self.stopped:
            return None
        e = self.E[eng]
        emb = self._wait(eng, self._deps(reads, writes), defer_last=self.EMBED)
        if e["sem"] is None:
            e["key"], e["sem"] = self.new_sem(eng)
            e["cnt"] = 0
        ins = fn(e["h"])
        if emb is not None:
            ins.wait_op(emb[0], emb[1], "sem-ge")
        if signal:
            e["cnt"] += 1
            ins.then_inc(e["sem"], 1)
            tok = (e["key"], e["sem"], e["cnt"], eng)
            if e["cnt"] >= self.LIMIT:
                e["sem"] = None
        else:
            tok = (e["key"], e["sem"], e["cnt"] + 1, eng)
        self._mark(tok, eng, reads, writes)
        return tok

    def dma(self, queue, out, in_, reads=(), writes=(), is_store=False, **kw):
        if self.stopped:
            return None
        e = self.E[queue]
        if e["slots"] is None:
            e["slots"] = []
            for i in range(self.NDMA):
                key, sem = self.new_sem(f"d{queue}")
                e["slots"].append({"key": key, "sem": sem, "cnt": 0, "last": None})
        slot = e["slots"][e["rr"] % self.NDMA]
        e["rr"] += 1
        deps = self._deps(reads, writes)
        deps.append(slot["last"])
        self._wait(queue, deps)
        ins = e["h"].dma_start(out=out, in_=in_, **kw)
        slot["cnt"] += 16
        ins.then_inc(slot["sem"], 16)
        tok = (slot["key"], slot["sem"], slot["cnt"], "dma")
        slot["last"] = tok
        self._mark(tok, slot["key"], reads, writes)
        if is_store:
            self.stores.append(tok)
        return tok

    def barrier(self):
        toks = []
        for name, e in self.E.items():
            if e["sem"] is not None and e["cnt"] > 0:
                toks.append((e["key"], e["sem"], e["cnt"], name))
            if e["slots"] is not None:
                for sl in e["slots"]:
                    if sl["last"] is not None:
                        toks.append(sl["last"])
        for name in self.E:
            e = self.E[name]
            best = {}
            for key, sem, val, src in toks:
                if e["seen"].get(key, 0) >= val:
                    continue
                if key not in best or best[key][1] < val:
                    best[key] = (sem, val)
            for key, (sem, val) in best.items():
                e["h"].wait_ge(sem, val)
                e["seen"][key] = val

    def collective(self, kind, ins, outs, groups, reads=(), writes=()):
        e = self.E["pool"]
        self._wait("pool", self._deps(reads, writes))
        key, sem = self.new_sem("cc")
        ins_ = e["h"].collective_compute(kind, ALU.bypass, replica_groups=groups,
                                         ins=[a.opt() for a in ins], outs=[a.opt() for a in outs])
        ins_.then_inc(sem)
        tok = (key, sem, 1, "dma")
        self.extra_toks.append(tok)
        self._mark(tok, key, reads, writes)
        return tok

    def finish(self):
        self.stopped = False
        self._wait("sp", self.stores)


class PsumPool:
    def __init__(self, nc, stack, n=8, tag=""):
        self.t = [stack.enter_context(nc.psum_tensor(f"ps{tag}{i}", [128, 512], F32)) for i in range(n)]
        self.b = [Buf(f"ps{i}", True) for i in range(n)]
        self.i = 0
        self.active = list(range(n))

    def get(self):
        i = self.active[self.i % len(self.active)]
        self.i += 1
        self.last = i
        return self.t[i], self.b[i]


TOK2 = B * S // NCORES
TT2 = 256


def emit_phase_c(nc, P, st, y_src, y_queue, xT, w_o, w_up, w_dn, g_ffn, oT, y_reads=(), pre=None):
    KD = D // 128
    KF = DFF // 128
    NT = TOK2 // TT2
    if True:
        sb = lambda name, shape, dt: st.enter_context(nc.sbuf_tensor("c_" + name, shape, dt))
        wo_sb = sb("wo_sb", [128, KD, D], BF16)
        wu_sb = sb("wu_sb", [128, KD, DFF], BF16)
        wd_sb = sb("wd_sb", [128, KF, D], BF16)
        g_sb = sb("g_sb", [128, KD], F32)
        ones = sb("ones", [128, 128], BF16)
        y_sb = [sb(f"y_sb{i}", [128, KD, TT2], BF16) for i in range(2)]
        x_sb = [sb(f"x_sb{i}", [128, KD, TT2], F32) for i in range(2)]
        h_sb = [sb(f"h_sb{i}", [128, KD, TT2], BF16) for i in range(2)]
        u_sb = sb("u_sb", [128, KF, TT2], BF16)
        sq_sb = [sb(f"sq_sb{i}", [128, TT2], BF16) for i in range(3)]
        r_sb = [sb(f"r_sb{i}", [128, TT2], F32) for i in range(3)]
        rstd = sb("rstd", [128, TT2], F32)
        ps = PsumPool(nc, st, 7, "c")
        ss_t = st.enter_context(nc.psum_tensor("c_ss_ps", [128, 512], F32))
        ss_b = Buf("ss", True)
        b_wo = [Buf() for _ in range(KD)]
        b_wu = [Buf() for _ in range(KD)]
        b_wd = [Buf() for _ in range(KF)]
        b_g, b_ones = Buf(), Buf()
        b_y = [Buf(), Buf()]
        b_x = [[Buf() for _ in range(KD)] for _ in range(2)]
        b_h = [[Buf() for _ in range(KD)] for _ in range(2)]
        b_u = [Buf() for _ in range(KF)]
        b_sq = [Buf(), Buf(), Buf()]
        b_r = [Buf(), Buf(), Buf()]
        b_rstd = Buf()

        P.op("dve", lambda e: e.memset(ones[:], 1.0), writes=[b_ones])
        P.dma("sp", g_sb[:], g_ffn, writes=[b_g])

        def load_tile(t):
            i = t % 2
            c0 = t * TT2
            P.dma(y_queue, y_sb[i][:], y_src(t), reads=(y_reads(t) if callable(y_reads) else list(y_reads)),
                  writes=[b_y[i]])
            P.dma("sp", x_sb[i][:], xT[:, c0:c0 + TT2].rearrange("(k p) c -> p k c", p=128),
                  writes=b_x[i])

        if pre is None:
            for k in range(KD):
                P.dma("pool", wo_sb[:, k, :], w_o[k * 128:(k + 1) * 128, :], writes=[b_wo[k]])
            load_tile(0)
            for k in range(KD):
                for hh in range(2):
                    P.dma("pool", wu_sb[:, k, hh * 2048:(hh + 1) * 2048],
                          w_up[k * 128:(k + 1) * 128, hh * 2048:(hh + 1) * 2048], writes=[b_wu[k]])
            for f in range(KF):
                P.dma("pool", wd_sb[:, f, :], w_dn[f * 128:(f + 1) * 128, :], writes=[b_wd[f]])
            wu_rd = lambda k, f: b_wu[k]
            wd_rd = lambda f: b_wd[f]
        else:
            wo_b, wu_b, wd_b, bo_, bu_, bd_ = pre
            P.dma("sp", wo_sb[:], wo_b.rearrange("(k p) n -> p k n", p=128), reads=[bo_], writes=[b_wo[0]])
            for k in range(1, KD):
                b_wo[k] = b_wo[0]
            load_tile(0)
            b_wug = [Buf() for _ in range(8)]
            b_wdg = [Buf() for _ in range(8)]
            for cg in range(8):
                P.dma("sp", wu_sb[:, :, cg * 512:(cg + 1) * 512],
                      wu_b[:, cg * 512:(cg + 1) * 512].rearrange("(k p) n -> p k n", p=128),
                      reads=[bu_], writes=[b_wug[cg]])
            for g_ in range(8):
                P.dma("sp", wd_sb[:, g_ * 4:(g_ + 1) * 4, :],
                      wd_b[g_ * 512:(g_ + 1) * 512, :].rearrange("(f p) n -> p f n", p=128),
                      reads=[bd_], writes=[b_wdg[g_]])
            wu_rd = lambda k, f: b_wug[f // 4]
            wd_rd = lambda f: b_wdg[f // 4]

        def prologue(t):
            i = t % 2
            xs, ys = x_sb[i], y_sb[i]
            lag = []

            def ss_mm(n, j):
                P.op("pe", lambda e: e.matmul(ss_t[:, 0:TT2], ones[:], sq_sb[j][:], start=(n == 0),
                                              stop=(n == KD - 1)),
                     reads=[b_ones, b_sq[j]], writes=[ss_b], signal=True)

            for n in range(KD):
                pt, pb = ps.get()
                for k in range(KD):
                    P.op("pe", lambda e, k=k, n=n, pt=pt: e.matmul(
                        pt[:, 0:TT2], wo_sb[:, k, n * 128:(n + 1) * 128], ys[:, k, :],
                        start=(k == 0), stop=(k == KD - 1)),
                        reads=[b_wo[k], b_y[i]], writes=[pb], signal=(k == KD - 1))
                P.op("dve", lambda e, n=n, pt=pt: e.tensor_tensor(
                    out=xs[:, n, :], in0=xs[:, n, :], in1=pt[:, 0:TT2], op=ALU.add),
                    reads=[pb, b_x[i][n]], writes=[b_x[i][n]])
                j = n % 3
                P.op("act", lambda e, n=n, j=j: e.activation(
                    out=sq_sb[j][:], in_=xs[:, n, :], func=AF.Square),
                    reads=[b_x[i][n]], writes=[b_sq[j]])
                lag.append((n, j))
                if len(lag) > 2:
                    ss_mm(*lag.pop(0))
            while lag:
                ss_mm(*lag.pop(0))
            P.op("act", lambda e: e.activation(out=rstd[:], in_=ss_t[:, 0:TT2], func=AF.Ln,
                                               scale=1.0 / D, bias=EPS),
                 reads=[ss_b], writes=[b_rstd])
            P.op("act", lambda e: e.activation(out=rstd[:], in_=rstd[:], func=AF.Exp, scale=-0.5),
                 reads=[b_rstd], writes=[b_rstd])
            for k in range(KD):
                P.op("dve", lambda e, k=k: e.scalar_tensor_tensor(
                    out=h_sb[i][:, k, :], in0=xs[:, k, :], scalar=g_sb[:, k:k + 1], in1=rstd[:],
                    op0=ALU.mult, op1=ALU.mult),
                    reads=[b_x[i][k], b_g, b_rstd], writes=[b_h[i][k]])

        def up(t):
            i = t % 2
            for f in range(KF):
                pt, pb = ps.get()
                for k in range(KD):
                    P.op("pe", lambda e, k=k, f=f, pt=pt: e.matmul(
                        pt[:, 0:TT2], wu_sb[:, k, f * 128:(f + 1) * 128], h_sb[i][:, k, :],
                        start=(k == 0), stop=(k == KD - 1)),
                        reads=[wu_rd(k, f), b_h[i][k]], writes=[pb], signal=(k == KD - 1))
                j = f % 3
                P.op("act", lambda e, j=j, pt=pt: e.activation(
                    out=r_sb[j][:], in_=pt[:, 0:TT2], func=AF.Relu),
                    reads=[pb], writes=[b_r[j]])
                P.op("dve", lambda e, j=j, f=f, pt=pt: e.tensor_tensor(
                    out=u_sb[:, f, :], in0=r_sb[j][:], in1=pt[:, 0:TT2], op=ALU.mult),
                    reads=[pb, b_r[j]], writes=[b_u[f]])

        def down(t):
            i = t % 2
            xs = x_sb[i]
            for n in range(KD):
                pt, pb = ps.get()
                for f in range(KF):
                    P.op("pe", lambda e, f=f, n=n, pt=pt: e.matmul(
                        pt[:, 0:TT2], wd_sb[:, f, n * 128:(n + 1) * 128], u_sb[:, f, :],
                        start=(f == 0), stop=(f == KF - 1)),
                        reads=[wd_rd(f), b_u[f]], writes=[pb], signal=(f == KF - 1))
                P.op("dve", lambda e, n=n, pt=pt: e.tensor_tensor(
                    out=xs[:, n, :], in0=xs[:, n, :], in1=pt[:, 0:TT2], op=ALU.add),
                    reads=[pb, b_x[i][n]], writes=[b_x[i][n]])
            c0 = t * TT2
            P.dma("sp", oT[:, c0:c0 + TT2].rearrange("(k p) c -> p k c", p=128), xs[:],
                  reads=b_x[i], is_store=True)

        prologue(0)
        for t in range(NT):
            if t + 1 < NT:
                load_tile(t + 1)
            up(t)
            if t + 1 < NT:
                prologue(t + 1)
            down(t)


def build_l2():
    nc = bass.Bass("TRN2", target_bir_lowering=False)
    yT = nc.dram_tensor("yT", [D, TOK2], F32, kind="ExternalInput").ap()
    xT = nc.dram_tensor("xT", [D, TOK2], F32, kind="ExternalInput").ap()
    w_o = nc.dram_tensor("w_o", [D, D], F32, kind="ExternalInput").ap()
    w_up = nc.dram_tensor("w_up", [D, DFF], F32, kind="ExternalInput").ap()
    w_dn = nc.dram_tensor("w_down", [DFF, D], F32, kind="ExternalInput").ap()
    g_ffn = nc.dram_tensor("g_ffn", [128, D // 128], F32, kind="ExternalInput").ap()
    oT = nc.dram_tensor("oT", [D, TOK2], F32, kind="ExternalOutput").ap()
    with ExitStack() as st:
        st.enter_context(nc.Block())
        P = Prog(nc, st)
        y_src = lambda t: yT[:, t * TT2:(t + 1) * TT2].rearrange("(k p) c -> p k c", p=128)
        emit_phase_c(nc, P, st, y_src, "pool", xT, w_o, w_up, w_dn, g_ffn, oT)
        P.finish()
    return nc


def run_l2(yT_full, xT_full, w_o, w_up, w_down, g_ffn):
    nc = build_l2()
    g = np.ascontiguousarray(g_ffn.reshape(D // 128, 128).T)
    in_maps = []
    for c in range(NCORES):
        in_maps.append({"yT": yT_full[c], "xT": xT_full[c], "w_o": w_o, "w_up": w_up,
                        "w_down": w_down, "g_ffn": g})
    res = run_bass_kernel_spmd(nc, in_maps, core_ids=list(range(NCORES)))
    return [r["oT"] for r in res.results]


TT = 512
NTILE = S // TT
NW1 = 12 * 128 + 80 + 6 * 128
NCOL = 38
SC_MLA = 192.0 ** -0.5
SC_MEM = 64.0 ** -0.5
SC_GLA = 128.0 ** -0.5


def _choff(i):
    if i < 12:
        return 128 * i, 128
    if i == 12:
        return 1536, 80
    return 1616 + 128 * (i - 13), 128


class SbPool:
    def __init__(self, nc, stack, name, shape, dt, n):
        self.t = [stack.enter_context(nc.sbuf_tensor(f"{name}{i}", shape, dt)) for i in range(n)]
        self.b = [Buf(f"{name}{i}") for i in range(n)]
        self.i = 0

    def get(self):
        i = self.i % len(self.t)
        self.i += 1
        return self.t[i], self.b[i]


class _Stop(Exception):
    pass


def build_l1(ntile=NTILE, stop=None, fused=False):
    nc = bass.Bass("TRN2", target_bir_lowering=False)

    def ck(n):
        if stop is not None and n >= stop:
            P.stopped = True
    din = lambda name, shape, dt=F32: nc.dram_tensor(name, shape, dt, kind="ExternalInput").ap()
    xT = din("xT", [D, S])
    pos = din("pos", [1, S], I32)
    w1 = din("w1", [D, NW1])
    w1t = din("w1t", [D, 256])
    wukv = din("wukv", [256, 512])
    wg = din("wg", [16, 128])
    memT = din("memT", [D, 256])
    wmem = din("wmem", [D, 320])
    cols_d = din("cols", [128, NCOL])
    tri_d = din("tri", [128, 128])
    ident_d = din("ident", [128, 128])
    if fused:
        xq = din("xq", [D, TOK2])
        w_o = din("w_o", [D, D])
        w_up = din("w_up", [D, DFF])
        w_dn = din("w_down", [DFF, D])
        g_ffn = din("g_ffn", [128, D // 128])
        oT = nc.dram_tensor("oT", [D, TOK2], F32, kind="ExternalOutput").ap()
        wo_b = nc.dram_tensor("wo_b", [D, D], BF16).ap()
        wu_b = nc.dram_tensor("wu_b", [D, DFF], BF16).ap()
        wd_b = nc.dram_tensor("wd_b", [DFF, D], BF16).ap()
        precast = ([(wo_b[k * 128:(k + 1) * 128, :], w_o[k * 128:(k + 1) * 128, :], "wo_b") for k in range(8)]
                   + [(wu_b[k * 128:(k + 1) * 128, hh * 2048:(hh + 1) * 2048],
                       w_up[k * 128:(k + 1) * 128, hh * 2048:(hh + 1) * 2048], "wu_b")
                      for k in range(8) for hh in range(2)]
                   + [(wd_b[k * 128:(k + 1) * 128, :], w_dn[k * 128:(k + 1) * 128, :], "wd_b") for k in range(32)])
        ybuf = [nc.dram_tensor(f"ybuf{q}", [256, TOK2], BF16).ap() for q in range(4)]
        ygath = nc.dram_tensor("ygath", [4 * D, TOK2], BF16).ap()
    else:
        yT = nc.dram_tensor("yT", [256, S], F32, kind="ExternalOutput").ap()
    KD = D // 128
    st0 = ExitStack()
    st0.enter_context(nc.Block())
    P = Prog(nc, st0)
    with ExitStack() as st:
        sb = lambda name, shape, dt: st.enter_context(nc.sbuf_tensor(name, shape, dt))
        w1_sb = sb("w1_sb", [128, KD, NW1], BF16)
        w1t_sb = sb("w1t_sb", [128, KD, 256], BF16)
        wukv_sb = sb("wukv_sb", [128, 2, 512], BF16)
        wg_sb = sb("wg_sb", [128, 128], BF16)
        gw_sb = sb("gw_sb", [128, 8 * TT], BF16)
        cols = sb("cols_sb", [128, NCOL], F32)
        ncols = sb("ncols_sb", [128, NCOL], F32)
        tri = sb("tri_sb", [128, 128], BF16)
        ident = sb("ident_sb", [128, 128], BF16)
        ones = sb("ones_sb", [128, 128], BF16)
        bd64 = sb("bd64_sb", [128, 128], BF16)
        onesf = sb("onesf_sb", [128, 128], F32)
        cache = sb("cache", [128, 5 * S], BF16)
        kn_c = cache[:, 0:2 * S].rearrange("p (h s) -> p h s", h=2)
        kpe_c = cache[:, 2 * S:3 * S]
        v_c = cache[:, 3 * S:5 * S].rearrange("p (b e) -> p b e", e=256)
        stg = cache[:].bitcast(F32)
        x_sb = sb("x_sb", [128, KD, TT], F32)
        h_sb = sb("h_sb", [128, KD, TT], BF16)
        qn_sb = sb("qn_sb", [128, 2, TT], BF16)
        qpe_sb = sb("qpe_sb", [128, 2, TT], BF16)
        ckvn_sb = sb("ckvn_sb", [128, 2, TT], BF16)
        qg_sb = sb("qg_sb", [128, TT], BF16)
        kg_sb = sb("kg_sb", [128, TT], BF16)
        kgf_sb = sb("kgf_sb", [128, TT], F32)
        cum_sb = sb("cum_sb", [128, TT], F32)
        vg_sb = sb("vg_sb", [128, 4, 256], BF16)
        qa_sb = sb("qa_sb", [128, TT], BF16)
        og_sb = sb("og_sb", [128, 2, TT], F32)
        omla_sb = sb("omla_sb", [128, 2, TT], F32)
        S_sb = sb("S_sb", [128, 256], F32)
        Sb_sb = sb("Sb_sb", [128, 256], BF16)
        kmT_sb = sb("kmT_sb", [128, 256], BF16)
        vm_sb = sb("vm_sb", [128, 2, 256], BF16)
        cs_sb = sb("cs_sb", [128, TT], F32)
        sn_sb = sb("sn_sb", [128, TT], F32)
        small = sb("small_sb", [128, 16], F32)
        FP = SbPool(nc, st, "fp", [128, TT], F32, 5)
        HP = SbPool(nc, st, "hp", [128, TT], BF16, 3)
        PT = SbPool(nc, st, "pt", [128, TT], BF16, 4)
        ps = PsumPool(nc, st, 7)
        o_ps = [ps.t[3], ps.t[4]]
        l_ps = [ps.t[5], ps.t[6]]
        tp_ps = st.enter_context(nc.psum_tensor("tp_ps", [128, 1024], BF16))
        b_tp = Buf("tp", True)
        ps.t.append(tp_ps[:].bitcast(F32))
        ps.b.append(b_tp)
        b_o = [ps.b[3], ps.b[4]]
        b_l = [ps.b[5], ps.b[6]]

        B_ = {}

        def bf(name):
            if name not in B_:
                B_[name] = Buf(name)
            return B_[name]

        P.dma("sp", cols[:], cols_d, writes=[bf("cols")])
        P.op("dve", lambda e: e.tensor_scalar(out=ncols[:], in0=cols[:], scalar1=-1.0, scalar2=None,
                                              op0=ALU.mult), reads=[bf("cols")], writes=[bf("ncols")])
        P.op("dve", lambda e: e.memset(ones[:], 1.0), writes=[bf("ones")])
        P.op("dve", lambda e: e.memset(onesf[:], 1.0), writes=[bf("onesf")])
        P.op("dve", lambda e: e.memset(bd64[:], 0.0), writes=[bf("bd64")])
        P.op("dve", lambda e: e.memset(bd64[0:64, 0:64], 1.0), writes=[bf("bd64")])
        P.op("dve", lambda e: e.memset(bd64[64:128, 64:128], 1.0), writes=[bf("bd64")])
        P.op("dve", lambda e: e.memset(S_sb[:], 0.0), writes=[bf("S")])
        P.op("dve", lambda e: e.memset(Sb_sb[:], 0.0), writes=[bf("Sb")])
        P.op("dve", lambda e: e.memset(qpe_sb[:], 0.0), writes=[bf("qpe")])
        P.dma("pool", tri[:], tri_d, writes=[bf("tri")])
        P.dma("pool", ident[:], ident_d, writes=[bf("ident")])
        P.dma("pool", wg_sb[64:80, :], wg, writes=[bf("wg")])
        for k in range(KD):
            P.dma("pool", gw_sb[:, k * 320:(k + 1) * 320], wmem[k * 128:(k + 1) * 128, :], writes=[bf("gw")])
        for k in range(2):
            P.dma("pool", wukv_sb[:, k, :], wukv[k * 128:(k + 1) * 128, :], writes=[bf("wukv")])
        for k in range(KD):
            P.dma("sp", stg[:, k * NW1:(k + 1) * NW1], w1[k * 128:(k + 1) * 128, :], writes=[bf(f"stg{k}")])
        for k in range(KD):
            if k % 2 == 0:
                P.op("dve", lambda e, k=k: e.tensor_copy(out=w1_sb[:, k, :], in_=stg[:, k * NW1:(k + 1) * NW1]),
                     reads=[bf(f"stg{k}")], writes=[bf(f"w1_{k}")])
            else:
                P.op("act", lambda e, k=k: e.activation(out=w1_sb[:, k, :], in_=stg[:, k * NW1:(k + 1) * NW1],
                                                        func=AF.Copy),
                     reads=[bf(f"stg{k}")], writes=[bf(f"w1_{k}")])
        for k in range(KD):
            P.dma("pool", w1t_sb[:, k, :], w1t[k * 128:(k + 1) * 128, :], writes=[bf("w1t")])

        def col(i, rows=128):
            return cols[0:rows, i:i + 1]

        def ncol(i, rows=128):
            return ncols[0:rows, i:i + 1]

        def mm(out_ap, out_buf, items, extra_reads=()):
            n = len(items)
            for j, (l, r, rd) in enumerate(items):
                P.op("pe", lambda e, l=l, r=r, j=j: e.matmul(out_ap, l, r, start=(j == 0), stop=(j == n - 1)),
                     reads=list(rd) + list(extra_reads), writes=[out_buf], signal=(j == n - 1))

        PSG = [ps]

        def rstd_begin(srcs, rows, N=TT):
            hs = []
            for ap, b in srcs:
                hq, hb = HP.get()
                P.op("act", lambda e, ap=ap, hq=hq: e.activation(out=hq[0:rows, 0:N], in_=ap, func=AF.Square),
                     reads=[b], writes=[hb])
                hs.append((hq, hb))
            return hs

        def rstd_end(hs, lhsT, lbuf, rows, dim, N=TT):
            pt, pb = PSG[0].get()
            n = len(hs)
            for j, (hq, hb) in enumerate(hs):
                P.op("pe", lambda e, hq=hq, j=j: e.matmul(pt[0:rows, 0:N], lhsT, hq[0:rows, 0:N],
                                                          start=(j == 0), stop=(j == n - 1)),
                     reads=[hb, lbuf], writes=[pb], signal=True)
            ft, fb = FP.get()
            P.op("act", lambda e: e.activation(out=ft[0:rows, 0:N], in_=pt[0:rows, 0:N], func=AF.Ln,
                                               scale=1.0 / dim, bias=EPS), reads=[pb], writes=[fb])
            P.op("act", lambda e: e.activation(out=ft[0:rows, 0:N], in_=ft[0:rows, 0:N], func=AF.Exp,
                                               scale=-0.5), reads=[fb], writes=[fb])
            return ft, fb

        def rstd_of(srcs, lhsT, lbuf, rows, dim, N=TT):
            pt, pb = PSG[0].get()
            n = len(srcs)
            for j, (ap, b) in enumerate(srcs):
                hq, hb = HP.get()
                P.op("act", lambda e, ap=ap, hq=hq: e.activation(out=hq[0:rows, 0:N], in_=ap, func=AF.Square),
                     reads=[b], writes=[hb])
                P.op("pe", lambda e, hq=hq, j=j: e.matmul(pt[0:rows, 0:N], lhsT, hq[0:rows, 0:N],
                                                          start=(j == 0), stop=(j == n - 1)),
                     reads=[hb, lbuf], writes=[pb], signal=True)
            ft, fb = FP.get()
            P.op("act", lambda e: e.activation(out=ft[0:rows, 0:N], in_=pt[0:rows, 0:N], func=AF.Ln,
                                               scale=1.0 / dim, bias=EPS), reads=[pb], writes=[fb])
            P.op("act", lambda e: e.activation(out=ft[0:rows, 0:N], in_=ft[0:rows, 0:N], func=AF.Exp,
                                               scale=-0.5), reads=[fb], writes=[fb])
            return ft, fb

        def sigmoid_to(out_ap, out_buf, in_ap, in_buf, nbias_ap, rows=128, N=TT):
            ft, fb = FP.get()
            if nbias_ap is None:
                P.op("act", lambda e: e.activation(out=ft[0:rows, 0:N], in_=in_ap, func=AF.Exp, scale=-1.0),
                     reads=[in_buf], writes=[fb])
            else:
                P.op("act", lambda e: e.activation(out=ft[0:rows, 0:N], in_=in_ap, func=AF.Exp, scale=-1.0,
                                                   bias=nbias_ap), reads=[in_buf, bf("ncols")], writes=[fb])
            P.op("act", lambda e: e.activation(out=ft[0:rows, 0:N], in_=ft[0:rows, 0:N], func=AF.Ln, bias=1.0),
                 reads=[fb], writes=[fb])
            P.op("act", lambda e: e.activation(out=out_ap, in_=ft[0:rows, 0:N], func=AF.Exp, scale=-1.0),
                 reads=[fb], writes=[out_buf])

        def recip_to(out_ap, out_buf, in_ap, in_buf, rows=128, N=TT):
            ft, fb = FP.get()
            P.op("act", lambda e: e.activation(out=ft[0:rows, 0:N], in_=in_ap, func=AF.Ln),
                 reads=[in_buf], writes=[fb])
            P.op("act", lambda e: e.activation(out=out_ap, in_=ft[0:rows, 0:N], func=AF.Exp, scale=-1.0),
                 reads=[fb], writes=[out_buf])

        def wchunk(k, i):
            o, w = _choff(i)
            return w1_sb[:, k, o:o + w], w

        def project(i, N=TT):
            o, w = _choff(i)
            pt, pb = PSG[0].get()
            mm(pt[0:w, 0:N], pb, [(w1_sb[:, k, o:o + w], h_sb[:, k, 0:N], [bf(f"w1_{k}"), bf(f"h{k}")])
                                  for k in range(KD)])
            return pt, pb

        ck(1)
        P.dma("sp", x_sb[:, :, 0:256], memT.rearrange("(k p) c -> p k c", p=128),
              writes=[bf(f"x{k}") for k in range(KD)])
        rm, rmb = rstd_of([(x_sb[:, k, 0:256], bf(f"x{k}")) for k in range(KD)], ones[:], bf("ones"),
                          128, float(D), N=256)
        for k in range(KD):
            P.op("dve", lambda e, k=k: e.scalar_tensor_tensor(
                out=h_sb[:, k, 0:256], in0=x_sb[:, k, 0:256], scalar=col(27 + k), in1=rm[:, 0:256],
                op0=ALU.mult, op1=ALU.mult), reads=[bf(f"x{k}"), bf("cols"), rmb], writes=[bf(f"h{k}")])
        pt, pb = ps.get()
        mm(pt[0:64, 0:256], pb, [(gw_sb[:, k * 320:k * 320 + 64], h_sb[:, k, 0:256], [bf("gw"), bf(f"h{k}")])
                                 for k in range(KD)])
        rk, rkb = rstd_of([(pt[0:64, 0:256], pb)], ones[0:64, 0:64], bf("ones"), 64, 64.0, N=256)
        P.op("dve", lambda e: e.scalar_tensor_tensor(
            out=kmT_sb[0:64, :], in0=pt[0:64, 0:256], scalar=col(17, 64), in1=rk[0:64, 0:256],
            op0=ALU.mult, op1=ALU.mult), reads=[pb, bf("cols"), rkb], writes=[bf("kmT")])
        for mb in range(2):
            pt, pb = ps.get()
            mm(pt[:, 0:256], pb, [(h_sb[:, k, mb * 128:(mb + 1) * 128], gw_sb[:, k * 320 + 64:(k + 1) * 320],
                                   [bf("gw"), bf(f"h{k}")]) for k in range(KD)])
            P.op("act", lambda e, mb=mb, pt=pt: e.activation(out=vm_sb[:, mb, :], in_=pt[:, 0:256], func=AF.Copy),
                 reads=[pb], writes=[bf("vm")])

        ck(2)
        TWO_PI = 2.0 * np.pi
        C1 = float(np.float32(6.28125))
        C2 = float(TWO_PI - 6.28125)
        PI_C = 3.1415925
        ALLB = [0, 1, 2, 3, 5]

        def gtile(j):
            return gw_sb[:, j * TT:(j + 1) * TT]

        def load_x(T):
            t0 = T * TT
            P.dma("sp", x_sb[:], xT[:, t0:t0 + TT].rearrange("(k p) c -> p k c", p=128),
                  writes=[bf(f"x{k}") for k in range(KD)])

        def normed(pt, pb, rt, rtb, gcol, out_ap, out_buf, rows=128):
            P.op("dve", lambda e: e.scalar_tensor_tensor(
                out=out_ap, in0=pt[0:rows, 0:TT], scalar=gcol, in1=rt[0:rows, 0:TT],
                op0=ALU.mult, op1=ALU.mult), reads=[pb, rtb, bf("cols")], writes=[out_buf])

        def tables_dve(T):
            t0 = T * TT
            sni = sn_sb[:].bitcast(I32)
            P.dma("sp", sni, pos[0, t0:t0 + TT].partition_broadcast(128), writes=[bf("sn")])
            P.op("dve", lambda e: e.tensor_copy(out=cs_sb[:], in_=sni), reads=[bf("sn")], writes=[bf("cs")])
            P.op("dve", lambda e: e.tensor_scalar(out=cs_sb[:], in0=cs_sb[:], scalar1=col(35), scalar2=None,
                                                  op0=ALU.mult), reads=[bf("cs"), bf("cols")], writes=[bf("cs")])
            P.op("dve", lambda e: e.tensor_scalar(out=sni, in0=cs_sb[:], scalar1=1.0 / TWO_PI, scalar2=None,
                                                  op0=ALU.mult), reads=[bf("cs")], writes=[bf("sn")])
            P.op("dve", lambda e: e.tensor_copy(out=sn_sb[:], in_=sni), reads=[bf("sn")], writes=[bf("sn")])
            for cc in (C1, C2):
                P.op("dve", lambda e, cc=cc: e.scalar_tensor_tensor(
                    out=cs_sb[:], in0=sn_sb[:], scalar=-cc, in1=cs_sb[:], op0=ALU.mult, op1=ALU.add),
                    reads=[bf("sn"), bf("cs")], writes=[bf("cs")])
            P.op("dve", lambda e: e.tensor_scalar(out=sn_sb[:], in0=cs_sb[:], scalar1=float(np.pi),
                                                  scalar2=-TWO_PI, op0=ALU.is_gt, op1=ALU.mult),
                 reads=[bf("cs")], writes=[bf("sn")])
            P.op("dve", lambda e: e.tensor_tensor(out=cs_sb[:], in0=cs_sb[:], in1=sn_sb[:], op=ALU.add),
                 reads=[bf("cs"), bf("sn")], writes=[bf("cs")])
            P.op("dve", lambda e: e.tensor_scalar(out=cs_sb[:], in0=cs_sb[:], scalar1=PI_C, scalar2=-PI_C,
                                                  op0=ALU.min, op1=ALU.max), reads=[bf("cs")], writes=[bf("cs")])

        def tables_act(T):
            P.op("act", lambda e: e.activation(out=sn_sb[:], in_=cs_sb[:], func=AF.Sin),
                 reads=[bf("cs")], writes=[bf("sn")])
            P.op("act", lambda e: e.activation(out=cs_sb[:], in_=cs_sb[:], func=AF.Abs),
                 reads=[bf("cs")], writes=[bf("cs")])
            P.op("act", lambda e: e.activation(out=cs_sb[:], in_=cs_sb[:], func=AF.Sin, scale=-1.0,
                                               bias=ncol(37)), reads=[bf("cs"), bf("ncols")], writes=[bf("cs")])
            P.op("dve", lambda e: e.tensor_scalar(out=sn_sb[:], in0=sn_sb[:], scalar1=col(36), scalar2=None,
                                                  op0=ALU.mult), reads=[bf("sn"), bf("cols")], writes=[bf("sn")])

        xn = {}

        def xnorm_a(T):
            pt, pb = PSG[0].get()
            pending = []

            def red(j, hq, hb):
                P.op("pe", lambda e: e.matmul(pt[:, 0:TT], ones[:], hq[:, 0:TT], start=(j == 0),
                                              stop=(j == KD - 1)),
                     reads=[hb, bf("ones")], writes=[pb], signal=True)

            for j in range(KD):
                hq, hb = HP.get()
                P.op("act", lambda e, j=j, hq=hq: e.activation(out=hq[:, 0:TT], in_=x_sb[:, j, :], func=AF.Square),
                     reads=[bf(f"x{j}")], writes=[hb])
                pending.append((j, hq, hb))
                if len(pending) > 2:
                    red(*pending.pop(0))
                if j % 2 == 1:
                    yield
            while pending:
                red(*pending.pop(0))
            ft, fb = FP.get()
            P.op("act", lambda e: e.activation(out=ft[:], in_=pt[:, 0:TT], func=AF.Ln,
                                               scale=1.0 / float(D), bias=EPS), reads=[pb], writes=[fb])
            P.op("act", lambda e: e.activation(out=ft[:], in_=ft[:], func=AF.Exp, scale=-0.5),
                 reads=[fb], writes=[fb])
            xn["r"] = (ft, fb)

        def xnorm_b(T):
            rx, rxb = xn["r"]
            for k in range(KD):
                P.op("dve", lambda e, k=k: e.scalar_tensor_tensor(
                    out=h_sb[:, k, :], in0=x_sb[:, k, :], scalar=col(k), in1=rx[:],
                    op0=ALU.mult, op1=ALU.mult), reads=[bf(f"x{k}"), bf("cols"), rxb], writes=[bf(f"h{k}")])
            if T + 1 < ntile:
                load_x(T + 1)

        def kv1(T):
            p4, p4b = project(4)
            h4 = rstd_begin([(p4[:, 0:TT], p4b)], 128)
            p5, p5b = project(5)
            h5 = rstd_begin([(p5[:, 0:TT], p5b)], 128)
            yield
            rc, rcb = rstd_end(h4 + h5, ones[:], bf("ones"), 128, 256.0)
            normed(p4, p4b, rc, rcb, col(14), ckvn_sb[:, 0, :], bf("ckvn"))
            normed(p5, p5b, rc, rcb, col(15), ckvn_sb[:, 1, :], bf("ckvn"))

        def kv_k(T):
            t0 = T * TT
            pp, hh = [], []
            for h in range(2):
                pt, pb = PSG[0].get()
                mm(pt[:, 0:TT], pb, [(wukv_sb[:, k, h * 128:(h + 1) * 128], ckvn_sb[:, k, :],
                                      [bf("wukv"), bf("ckvn")]) for k in range(2)])
                pp.append((pt, pb))
                hh.append(rstd_begin([(pt[:, 0:TT], pb)], 128))
            yield
            for h in range(2):
                pt, pb = pp[h]
                rt, rtb = rstd_end(hh[h], ones[:], bf("ones"), 128, 128.0)
                normed(pt, pb, rt, rtb, col(9), kn_c[:, h, t0:t0 + TT], bf(f"kn{h}_{T}"))

        def kv_v(T):
            for blk in range(4):
                pt, pb = PSG[0].get()
                mm(pt[:, 0:256], pb, [(ckvn_sb[:, k, blk * 128:(blk + 1) * 128], wukv_sb[:, k, 256:512],
                                       [bf("wukv"), bf("ckvn")]) for k in range(2)])
                P.op("act", lambda e, blk=blk, pt=pt: e.activation(out=v_c[:, T * 4 + blk, :], in_=pt[:, 0:256],
                                                                  func=AF.Copy),
                     reads=[pb], writes=[bf(f"v{T * 4 + blk}")])

        def rope_finish(pa, pab, pb_, pbb, rr, rrb, ga, gb, out_ap, out_buf):
            fa, fab = FP.get()
            normed(pa, pab, rr, rrb, col(ga), fa[:], fab)
            P.op("dve", lambda e: e.tensor_tensor(out=fa[:], in0=fa[:], in1=cs_sb[:], op=ALU.mult),
                 reads=[fab, bf("cs")], writes=[fab])
            fb_, fbb = FP.get()
            normed(pb_, pbb, rr, rrb, col(gb), fb_[:], fbb)
            P.op("dve", lambda e: e.tensor_tensor(out=fb_[:], in0=fb_[:], in1=sn_sb[:], op=ALU.mult),
                 reads=[fbb, bf("sn")], writes=[fbb])
            if out_ap is None:
                for h in range(2):
                    rs_ = slice(64 * h, 64 * h + 64)
                    P.op("dve", lambda e, h=h, rs_=rs_: e.tensor_tensor(out=qpe_sb[rs_, h, :], in0=fa[rs_, :],
                                                                      in1=fb_[rs_, :], op=ALU.add),
                         reads=[fab, fbb], writes=[out_buf])
            else:
                P.op("dve", lambda e: e.tensor_tensor(out=out_ap, in0=fa[:], in1=fb_[:], op=ALU.add),
                     reads=[fab, fbb], writes=[out_buf])

        def kv_rope(T):
            t0 = T * TT
            pa, pab = project(6)
            ha = rstd_begin([(pa[:, 0:TT], pab)], 128)
            pb_, pbb = project(7)
            yield
            rr, rrb = rstd_end(ha, bd64[:], bf("bd64"), 128, 64.0)
            rope_finish(pa, pab, pb_, pbb, rr, rrb, 12, 13, kpe_c[:, t0:t0 + TT], bf(f"kpe_{T}"))

        def next_tile_stages(T):
            return [lambda: tables_dve(T), lambda: xnorm_a(T), lambda: xnorm_b(T), lambda: tables_act(T),
                    lambda: kv1(T), lambda: kv_k(T), lambda: kv_v(T), lambda: kv_rope(T)]

        def pro_q(T):
            p0, p0b = project(0)
            h0 = rstd_begin([(p0[:, 0:TT], p0b)], 128)
            p1, p1b = project(1)
            h1 = rstd_begin([(p1[:, 0:TT], p1b)], 128)
            r0, r0b = rstd_end(h0, ones[:], bf("ones"), 128, 128.0)
            normed(p0, p0b, r0, r0b, col(8), qn_sb[:, 0, :], bf("qn0"))
            p2, p2b = project(2)
            h2 = rstd_begin([(p2[:, 0:TT], p2b)], 128)
            r1, r1b = rstd_end(h1, ones[:], bf("ones"), 128, 128.0)
            normed(p1, p1b, r1, r1b, col(8), qn_sb[:, 1, :], bf("qn1"))
            p3, p3b = project(3)
            rr, rrb = rstd_end(h2, bd64[:], bf("bd64"), 128, 64.0)
            rope_finish(p2, p2b, p3, p3b, rr, rrb, 10, 11, None, bf("qpe"))

        def pro_rest(T):
            ck(6)
            pq, pqb = project(12)
            hq_ = rstd_begin([(pq[0:64, 0:TT], pqb)], 64)
            P.op("act", lambda e: e.activation(out=qa_sb[64:80, :], in_=pq[64:80, 0:TT], func=AF.Copy),
                 reads=[pqb], writes=[bf("ag")])
            p8, p8b = project(8)
            i8 = ps.last
            rq, rqb = rstd_end(hq_, ones[0:64, 0:64], bf("ones"), 64, 64.0)
            P.op("dve", lambda e: e.scalar_tensor_tensor(
                out=qa_sb[0:64, :], in0=pq[0:64, 0:TT], scalar=col(16, 64), in1=rq[0:64, :],
                op0=ALU.mult, op1=ALU.mult), reads=[pqb, rqb, bf("cols")], writes=[bf("qm")])
            p9, p9b = project(9)
            i9 = ps.last
            ps.active = [i for i in ALLB if i not in (i8, i9)]
            pz, pzb = ps.get()
            mm(pz[:, 0:TT], pzb, [(wg_sb[64:80, :], qa_sb[64:80, :], [bf("wg"), bf("ag")])])
            sp_, spb = FP.get()
            P.op("act", lambda e: e.activation(out=sp_[:], in_=pz[:, 0:TT], func=AF.Exp, scale=-1.0,
                                               bias=ncol(24)), reads=[pzb, bf("ncols")], writes=[spb])
            P.op("act", lambda e: e.activation(out=sp_[:], in_=sp_[:], func=AF.Ln, bias=1.0),
                 reads=[spb], writes=[spb])
            for n in range(4):
                P.op("dve", lambda e, n=n: e.tensor_tensor_scan(
                    out=cum_sb[:, n * 128:(n + 1) * 128], data0=onesf[:], data1=sp_[:, n * 128:(n + 1) * 128],
                    initial=0.0, op0=ALU.mult, op1=ALU.add), reads=[spb, bf("onesf")], writes=[bf("cum")])
            for blk in range(4):
                pt, pb = ps.get()
                mm(pt[:, 0:256], pb, [(h_sb[:, k, blk * 128:(blk + 1) * 128], w1t_sb[:, k, :],
                                       [bf("w1t"), bf(f"h{k}")]) for k in range(KD)])
                P.op("act", lambda e, blk=blk, pt=pt: e.activation(out=vg_sb[:, blk, :], in_=pt[:, 0:256],
                                                                  func=AF.Copy),
                     reads=[pb], writes=[bf(f"vg{blk}")])
            for j in range(6):
                pt, pb = project(13 + j)
                sigmoid_to(gtile(j), bf(f"g{j}"), pt[:, 0:TT], pb, ncol(18 + j))
            epos, eposb = FP.get()
            eneg, enegb = FP.get()
            P.op("act", lambda e: e.activation(out=epos[:], in_=cum_sb[:], func=AF.Exp, scale=1.0 / 16),
                 reads=[bf("cum")], writes=[eposb])
            P.op("act", lambda e: e.activation(out=eneg[:], in_=cum_sb[:], func=AF.Exp, scale=-1.0 / 16),
                 reads=[bf("cum")], writes=[enegb])
            P.op("dve", lambda e: e.scalar_tensor_tensor(
                out=qg_sb[:], in0=p8[:, 0:TT], scalar=SC_GLA, in1=eneg[:], op0=ALU.mult, op1=ALU.mult),
                reads=[p8b, enegb], writes=[bf("qg")])
            P.op("dve", lambda e: e.tensor_tensor(out=kg_sb[:], in0=p9[:, 0:TT], in1=epos[:], op=ALU.mult),
                 reads=[p9b, eposb], writes=[bf("kg")])
            P.op("act", lambda e: e.activation(out=kgf_sb[:], in_=p9[:, 0:TT], func=AF.Copy),
                 reads=[p9b], writes=[bf("kgf")])
            ps.active = ALLB
            for j in range(2):
                pt, pb = project(10 + j)
                sg, sgb = FP.get()
                sigmoid_to(sg[:], sgb, pt[:, 0:TT], pb, None)
                P.op("dve", lambda e, j=j, pt=pt, sg=sg: e.tensor_tensor(
                    out=gtile(6 + j), in0=pt[:, 0:TT], in1=sg[:], op=ALU.mult),
                    reads=[pb, sgb], writes=[bf(f"g{6 + j}")])

        ps2 = PsumPool.__new__(PsumPool)
        ps2.t, ps2.b, ps2.i, ps2.active = ps.t, ps.b, 0, [4, 6, 2]

        cur_gen = [None]

        def run_stage(stages, h):
            ps2.active = [4, 6, 7] if h == 0 else [3, 5, 7]
            PSG[0] = ps2
            if cur_gen[0] is None:
                r = stages.pop(0)()
                if r is not None and hasattr(r, "__next__"):
                    cur_gen[0] = r
            if cur_gen[0] is not None:
                try:
                    next(cur_gen[0])
                except StopIteration:
                    cur_gen[0] = None
            PSG[0] = ps

        def stages_left(stages):
            return bool(stages) or cur_gen[0] is not None

        def attention(T, h, stages):
            nkb = 4 * T + 4
            ob, lb = b_o[h], b_l[h]
            ot, lt = o_ps[h], l_ps[h]
            pend = []
            stages = list(stages)
            nslots = len(stages) + (8 if h == 0 else 0)
            gap = max(1, nkb // (nslots + 1)) if stages else 1

            def score(kb):
                jd = kb - 4 * T
                c0 = 128 * jd if jd > 0 else 0
                pt, pb = ps.get()
                mm(pt[:, c0:TT], pb, [
                    (kn_c[:, h, kb * 128:(kb + 1) * 128], qn_sb[:, h, c0:TT],
                     [bf(f"kn{h}_{kb // 4}"), bf(f"qn{h}")]),
                    (kpe_c[:, kb * 128:(kb + 1) * 128], qpe_sb[:, h, c0:TT],
                     [bf(f"kpe_{kb // 4}"), bf("qpe")])])
                pT, pTb = PT.get()
                P.op("act", lambda e: e.activation(out=pT[:, c0:TT], in_=pt[:, c0:TT], func=AF.Exp,
                                                   scale=SC_MLA), reads=[pb], writes=[pTb])
                if jd >= 0:
                    P.op("pool", lambda e: e.tensor_tensor(out=pT[:, c0:c0 + 128], in0=pT[:, c0:c0 + 128],
                                                           in1=tri[:], op=ALU.mult),
                         reads=[pTb, bf("tri")], writes=[pTb])
                return (kb, c0, pT, pTb)

            def pv(item):
                kb, c0, pT, pTb = item
                first, last = (kb == 0), (kb == nkb - 1)
                P.op("pe", lambda e: e.matmul(ot[:, c0:TT], v_c[:, kb, h * 128:(h + 1) * 128], pT[:, c0:TT],
                                              start=first, stop=last),
                     reads=[bf(f"v{kb}"), pTb], writes=[ob], signal=last)
                P.op("pe", lambda e: e.matmul(lt[:, c0:TT], ones[:], pT[:, c0:TT], start=first, stop=last),
                     reads=[bf("ones"), pTb], writes=[lb], signal=True)

            LOOK = 3
            for kb in range(nkb):
                pend.append(score(kb))
                if len(pend) > LOOK:
                    pv(pend.pop(0))
                if stages_left(stages) and kb % gap == gap - 1:
                    run_stage(stages, h)
            while pend:
                pv(pend.pop(0))
            rl, rlb = FP.get()
            recip_to(rl[:], rlb, lt[:, 0:TT], lb)
            P.op("dve", lambda e: e.tensor_tensor(out=omla_sb[:, h, :], in0=ot[:, 0:TT], in1=rl[:], op=ALU.mult),
                 reads=[ob, rlb], writes=[bf(f"omla{h}")])
            P.op("pool", lambda e: e.tensor_tensor(out=omla_sb[:, h, :], in0=omla_sb[:, h, :], in1=gtile(h),
                                                   op=ALU.mult),
                 reads=[bf(f"omla{h}"), bf(f"g{h}")], writes=[bf(f"omla{h}")])
            while stages_left(stages):
                run_stage(stages, h)

        def gla_stages(T):
            out = []
            for n in range(4):
                st_ = {}
                cs_ = slice(n * 128, (n + 1) * 128)
                nb = small[:, 2 * n:2 * n + 1]
                dc = small[:, 2 * n + 1:2 * n + 2]

                def s1(n=n, st_=st_, cs_=cs_, nb=nb, dc=dc):
                    P.op("dve", lambda e: e.tensor_scalar(
                        out=nb, in0=cum_sb[:, n * 128 + 127:n * 128 + 128], scalar1=-1.0 / 16, scalar2=None,
                        op0=ALU.mult), reads=[bf("cum")], writes=[bf(f"nb{n}")])
                    P.op("act", lambda e: e.activation(out=dc, in_=nb, func=AF.Exp),
                         reads=[bf(f"nb{n}")], writes=[bf(f"dc{n}")])
                    kd, kdb = FP.get()
                    P.op("act", lambda e: e.activation(out=kd[:, 0:128], in_=cum_sb[:, cs_], func=AF.Exp,
                                                       scale=1.0 / 16, bias=nb),
                         reads=[bf("cum"), bf(f"nb{n}")], writes=[kdb])
                    kdT, kdTb = HP.get()
                    P.op("dve", lambda e: e.tensor_tensor(out=kdT[:, 0:128], in0=kgf_sb[:, cs_], in1=kd[:, 0:128],
                                                          op=ALU.mult), reads=[bf("kgf"), kdb], writes=[kdTb])
                    st_["kdT"] = (kdT, kdTb)

                def s2(n=n, st_=st_, cs_=cs_):
                    kdT, kdTb = st_["kdT"]
                    P.op("pe", lambda e: e.transpose(tp_ps[:, 0:128], kdT[:, 0:128], ident[:]),
                         reads=[kdTb, bf("ident")], writes=[b_tp])
                    kdk, kdkb = HP.get()
                    P.op("act", lambda e: e.activation(out=kdk[:, 0:128], in_=tp_ps[:, 0:128], func=AF.Copy),
                         reads=[b_tp], writes=[kdkb])
                    pa, pab = PSG[0].get()
                    mm(pa[:, 0:128], pab, [(kg_sb[:, cs_], qg_sb[:, cs_], [bf("kg"), bf("qg")])])
                    am, amb = HP.get()
                    P.op("dve", lambda e: e.tensor_tensor(out=am[:, 0:128], in0=pa[:, 0:128], in1=tri[:],
                                                          op=ALU.mult), reads=[pab, bf("tri")], writes=[amb])
                    st_["kdk"] = (kdk, kdkb)
                    st_["am"] = (am, amb)

                def s3(n=n, st_=st_, cs_=cs_, dc=dc):
                    kdk, kdkb = st_["kdk"]
                    am, amb = st_["am"]
                    po, pob = PSG[0].get()
                    for eh in range(2):
                        mm(po[:, eh * 128:(eh + 1) * 128], pob, [
                            (vg_sb[:, n, eh * 128:(eh + 1) * 128], am[:, 0:128], [bf(f"vg{n}"), amb]),
                            (Sb_sb[:, eh * 128:(eh + 1) * 128], qg_sb[:, cs_], [bf("Sb"), bf("qg")])])
                    for eh in range(2):
                        P.op("act", lambda e, eh=eh: e.activation(
                            out=og_sb[:, eh, cs_], in_=po[:, eh * 128:(eh + 1) * 128], func=AF.Copy),
                            reads=[pob], writes=[bf(f"og{eh}")])
                    pd, pdb = PSG[0].get()
                    mm(pd[:, 0:256], pdb, [(kdk[:, 0:128], vg_sb[:, n, :], [kdkb, bf(f"vg{n}")])])
                    P.op("dve", lambda e: e.scalar_tensor_tensor(
                        out=S_sb[:], in0=S_sb[:], scalar=dc, in1=pd[:, 0:256], op0=ALU.mult, op1=ALU.add),
                        reads=[bf("S"), bf(f"dc{n}"), pdb], writes=[bf("S")])
                    P.op("act", lambda e: e.activation(out=Sb_sb[:], in_=S_sb[:], func=AF.Copy),
                         reads=[bf("S")], writes=[bf("Sb")])

                out += [s1, s2, s3]
            return out

        fin = {}

        def fin_a(T):
            rg, rgb = rstd_of([(og_sb[:, eh, :], bf(f"og{eh}")) for eh in range(2)], ones[:], bf("ones"),
                              128, 256.0)
            for eh in range(2):
                P.op("dve", lambda e, eh=eh: e.scalar_tensor_tensor(
                    out=og_sb[:, eh, :], in0=og_sb[:, eh, :], scalar=col(25 + eh), in1=rg[:],
                    op0=ALU.mult, op1=ALU.mult), reads=[bf(f"og{eh}"), rgb, bf("cols")], writes=[bf(f"og{eh}")])
                for j in (6 + eh, 2 + eh):
                    P.op("pool", lambda e, eh=eh, j=j: e.tensor_tensor(
                        out=og_sb[:, eh, :], in0=og_sb[:, eh, :], in1=gtile(j), op=ALU.mult),
                        reads=[bf(f"og{eh}"), bf(f"g{j}")], writes=[bf(f"og{eh}")])
            ck(9)
            pts = []
            for mb in range(2):
                pt, pb = ps.get()
                mm(pt[:, 0:TT], pb, [(kmT_sb[0:64, mb * 128:(mb + 1) * 128], qa_sb[0:64, :],
                                      [bf("kmT"), bf("qm")])])
                pT, pTb = PT.get()
                P.op("act", lambda e, pt=pt, pT=pT: e.activation(out=pT[:], in_=pt[:, 0:TT], func=AF.Exp,
                                                                 scale=SC_MEM), reads=[pb], writes=[pTb])
                pts.append((pT, pTb))
            fin["pts"] = pts

        def fin_b(T):
            t0 = T * TT
            pts = fin["pts"]
            pm_os = [(ps.t[4], ps.b[4]), (ps.t[6], ps.b[6])]
            pm_l, pm_lb = ps.get()
            for mb in range(2):
                pT, pTb = pts[mb]
                P.op("pe", lambda e, pT=pT, mb=mb: e.matmul(pm_l[:, 0:TT], ones[:], pT[:], start=(mb == 0),
                                                            stop=(mb == 1)),
                     reads=[bf("ones"), pTb], writes=[pm_lb], signal=True)
            rl, rlb = FP.get()
            recip_to(rl[:], rlb, pm_l[:, 0:TT], pm_lb)
            for eh in range(2):
                pm_o, pm_ob = pm_os[eh]
                for mb in range(2):
                    pT, pTb = pts[mb]
                    P.op("pe", lambda e, pT=pT, mb=mb, eh=eh, pm_o=pm_o: e.matmul(
                        pm_o[:, 0:TT], vm_sb[:, mb, eh * 128:(eh + 1) * 128], pT[:], start=(mb == 0),
                        stop=(mb == 1)), reads=[bf("vm"), pTb], writes=[pm_ob], signal=(mb == 1))
                om, omb = FP.get()
                P.op("dve", lambda e, om=om, pm_o=pm_o: e.tensor_tensor(out=om[:], in0=pm_o[:, 0:TT], in1=rl[:],
                                                                        op=ALU.mult),
                     reads=[pm_ob, rlb], writes=[omb])
                P.op("dve", lambda e, om=om, eh=eh: e.tensor_tensor(out=om[:], in0=om[:], in1=gtile(4 + eh),
                                                                   op=ALU.mult),
                     reads=[omb, bf(f"g{4 + eh}")], writes=[omb])
                ck(10)
                P.op("dve", lambda e, om=om, eh=eh: e.tensor_tensor(out=omla_sb[:, eh, :], in0=omla_sb[:, eh, :],
                                                                   in1=om[:], op=ALU.add),
                     reads=[bf(f"omla{eh}"), omb], writes=[bf(f"omla{eh}")])
                P.op("dve", lambda e, eh=eh: e.tensor_tensor(out=omla_sb[:, eh, :], in0=omla_sb[:, eh, :],
                                                             in1=og_sb[:, eh, :], op=ALU.add),
                     reads=[bf(f"omla{eh}"), bf(f"og{eh}")], writes=[bf(f"omla{eh}")])
            if fused:
                for _ in range(4 if T < 14 else 0):
                    o_, i_, nm = precast.pop(0)
                    P.dma("pool", o_, i_, writes=[bf(nm)])
                qq, qo = T // 4, (T % 4) * TT
                P.dma("pool", ybuf[qq][:, qo:qo + TT].rearrange("(j p) c -> p j c", p=128), omla_sb[:],
                      reads=[bf("omla0"), bf("omla1")], writes=[bf(f"ybuf{qq}")])
                if T % 4 == 3:
                    P.collective("AllGather", [ybuf[qq]], [ygath[qq * D:(qq + 1) * D, :]],
                                 [[0, 1, 2, 3], [4, 5, 6, 7]],
                                 reads=[bf(f"ybuf{qq}")], writes=[bf(f"ygath{qq}")])
            else:
                P.dma("sp", yT[:, t0:t0 + TT].rearrange("(j p) c -> p j c", p=128), omla_sb[:],
                      reads=[bf("omla0"), bf("omla1")], is_store=True)

        load_x(0)
        for f in next_tile_stages(0):
            r_ = f()
            if r_ is not None and hasattr(r_, "__next__"):
                for _ in r_:
                    pass
        pro_q(0)
        for T in range(ntile):
            ck(3)
            ps.active = ALLB
            pro_rest(T)
            ck(7)
            ps.active = [0, 1, 2]
            attention(T, 0, next_tile_stages(T + 1) if T + 1 < ntile else [])
            attention(T, 1, gla_stages(T))
            ck(8)
            ps.active = ALLB
            fin_a(T)
            if T + 1 < ntile:
                pro_q(T + 1)
            fin_b(T)
        if not fused:
            P.finish()
    if fused:
        P.barrier()
        with ExitStack() as st2:
            rank = nc.sync.partition_id() % 4
            ygq = [ygath[q * D:(q + 1) * D, :].rearrange("(k p) c -> p k c", p=128) for q in range(4)]
            y_src = lambda t: ygq[t // 2][:, :, bass.ds(rank * 512 + (t % 2) * TT2, TT2)]
            while precast:
                o_, i_, nm = precast.pop(0)
                P.dma("pool", o_, i_, writes=[bf(nm)])
            emit_phase_c(nc, P, st2, y_src, "sp", xq, w_o, w_up, w_dn, g_ffn, oT, y_reads=lambda t: [bf(f"ygath{t // 2}")],
                         pre=(wo_b, wu_b, wd_b, bf("wo_b"), bf("wu_b"), bf("wd_b")))
            P.finish()
    st0.close()
    return nc


def _sw(a):
    n = a.shape[-1] // 2
    return np.concatenate([a[..., n:], a[..., :n]], axis=-1)


def _tile128(v):
    return np.ascontiguousarray(v.reshape(-1, 128).T)


def prep_l1(inp, b, g):
    f32 = np.float32
    w_in = inp["w_in"][0]
    h0, h1 = 2 * g, 2 * g + 1
    qn = lambda h: w_in[:, 192 * h:192 * h + 128]
    qr = lambda h: w_in[:, 192 * h + 128:192 * h + 192]
    ckv = w_in[:, 1536:1792]
    kr = w_in[:, 1792:1856]
    qg = w_in[:, 1856 + 128 * g:1856 + 128 * g + 128]
    kg = w_in[:, 2368 + 128 * g:2368 + 128 * g + 128]
    vg = w_in[:, 2880 + 256 * g:2880 + 256 * g + 256]
    ag = w_in[:, 3904:3920]
    rg = w_in[:, 3920 + 256 * g:3920 + 256 * g + 256]
    qm = w_in[:, 4944 + 64 * g:4944 + 64 * g + 64]
    gate = lambda k: w_in[:, 5200 + 1024 * k + 256 * g:5200 + 1024 * k + 256 * g + 256]
    chunks = [qn(h0), qn(h1),
              np.concatenate([qr(h0), qr(h1)], 1), np.concatenate([_sw(qr(h0)), _sw(qr(h1))], 1),
              ckv[:, 0:128], ckv[:, 128:256],
              np.concatenate([kr, kr], 1), np.concatenate([_sw(kr), _sw(kr)], 1),
              qg, kg, rg[:, 0:128], rg[:, 128:256],
              np.concatenate([qm, ag], 1),
              gate(0), gate(1), gate(2)]
    w1 = np.ascontiguousarray(np.concatenate(chunks, 1), dtype=f32)
    assert w1.shape == (D, NW1), w1.shape
    w_ukv = inp["w_ukv"][0]
    kc = lambda h: w_ukv[:, 256 * h:256 * h + 128]
    vc = lambda h: w_ukv[:, 256 * h + 128:256 * h + 256]
    wukv = np.ascontiguousarray(np.concatenate([kc(h0), kc(h1), vc(h0), vc(h1)], 1), dtype=f32)
    wgm = np.ascontiguousarray(inp["w_gla_gate"][0][:, 128 * g:128 * g + 128], dtype=f32)
    wm = inp["w_mem_kv"][0]
    wmem = np.ascontiguousarray(np.concatenate([wm[:, 64 * g:64 * g + 64],
                                                wm[:, 256 + 256 * g:256 + 256 * g + 256]], 1), dtype=f32)
    cols = np.zeros((128, NCOL), f32)
    cols[:, 0:8] = _tile128(inp["g_mix"][0])
    cols[:, 8] = inp["g_q_nope"][0]
    cols[:, 9] = inp["g_k_nope"][0]
    gq, gk = inp["g_q_rope"][0], inp["g_k_rope"][0]
    cols[:, 10] = np.concatenate([gq, gq])
    cols[:, 11] = np.concatenate([_sw(gq), _sw(gq)])
    cols[:, 12] = np.concatenate([gk, gk])
    cols[:, 13] = np.concatenate([_sw(gk), _sw(gk)])
    cols[:, 14:16] = _tile128(inp["g_ckv"][0])
    cols[:, 16] = np.concatenate([inp["g_q_mem"][0]] * 2)
    cols[:, 17] = np.concatenate([inp["g_k_mem"][0]] * 2)
    bg = inp["b_gate"][0]
    for k in range(3):
        for j in range(2):
            o = 1024 * k + 256 * g + 128 * j
            cols[:, 18 + 2 * k + j] = bg[o:o + 128]
    cols[:, 24] = inp["b_gla_gate"][0][128 * g:128 * g + 128]
    cols[:, 25:27] = _tile128(inp["g_gla_out"][0])
    cols[:, 27:35] = _tile128(inp["g_mem"][0])
    inv = (1.0 / (10000.0 ** (np.arange(0, 64, 2, dtype=f32) / f32(64)))).astype(f32)
    cols[:, 35] = np.tile(inv, 4)
    cols[:, 36] = np.tile(np.concatenate([-np.ones(32, f32), np.ones(32, f32)]), 2)
    cols[:, 37] = f32(-np.pi / 2)
    tri = np.triu(np.ones((128, 128), f32))
    ident = np.eye(128, dtype=f32)
    return {
        "xT": np.ascontiguousarray(inp["x"][b].T),
        "pos": np.ascontiguousarray(inp["positions"][b:b + 1]).astype(np.int32),
        "w1": w1, "w1t": np.ascontiguousarray(vg, dtype=f32), "wukv": wukv, "wg": wgm,
        "memT": np.ascontiguousarray(inp["mem"][b].T), "wmem": wmem,
        "cols": cols, "tri": tri, "ident": ident,
    }


def kernel(**inp):
    inp = {k: np.asarray(v) for k, v in inp.items()}
    nc = build_l1(fused=True)
    w_o = np.ascontiguousarray(inp["w_o"][0])
    w_up = np.ascontiguousarray(inp["w_up"][0])
    w_dn = np.ascontiguousarray(inp["w_down"][0])
    g = _tile128(inp["g_ffn"][0])
    maps = []
    for c in range(NCORES):
        b, q = c // 4, c % 4
        m = prep_l1(inp, b, q)
        tok = np.concatenate([np.arange(Q * TOK2 + q * 512, Q * TOK2 + q * 512 + 512) for Q in range(4)])
        m["xq"] = np.ascontiguousarray(m["xT"][:, tok])
        m.update({"w_o": w_o, "w_up": w_up, "w_down": w_dn, "g_ffn": g})
        maps.append(m)
    res = run_bass_kernel_spmd(nc, maps, core_ids=list(range(NCORES)))
    out = np.empty((B, S, D), np.float32)
    for c in range(NCORES):
        b, q = c // 4, c % 4
        tok = np.concatenate([np.arange(Q * TOK2 + q * 512, Q * TOK2 + q * 512 + 512) for Q in range(4)])
        out[b, tok] = res.results[c]["oT"].T
    return out
```
